# Optimizing a Trainium2 kernel written in Bass

```python
import jax
import jax.numpy as jnp
from jax import lax
import numpy as np

D_MODEL = 1024
BATCH = 8
SEQ = 4096
DEPTH = 2

GRID_W = 64
CTX_LEN = 256
N_RET_HEADS = 4
RET_HEAD_DIM = 128
N_GLA_HEADS = 4
GLA_KEY_DIM = 64
GLA_VAL_DIM = 128
GLA_GATE_RANK = 16
GLA_GATE_NORM = 16.0
D_FF = 4 * D_MODEL
CHUNK = 64
ROPE_BASE = 10000.0
LN2 = 0.6931471805599453
EPS = 1e-6

RET_W = N_RET_HEADS * RET_HEAD_DIM
GLA_K = N_GLA_HEADS * GLA_KEY_DIM
GLA_V = N_GLA_HEADS * GLA_VAL_DIM
MIX_W = RET_W + GLA_V
IN_SPLITS = (RET_W, RET_W, RET_W, RET_W, GLA_K, GLA_K, GLA_V, GLA_V, GLA_GATE_RANK, GLA_GATE_RANK)
IN_W = 4 * RET_W + 2 * GLA_K + 2 * GLA_V + 2 * GLA_GATE_RANK

kernel_name = 'hybrid_retention_gla_dit_block'


def rmsnorm(x, g):
    xf = x.astype(jnp.float32)
    y = xf * lax.rsqrt(jnp.mean(xf * xf, axis=-1, keepdims=True) + EPS)
    return (y * g.astype(jnp.float32)).astype(x.dtype)


def head_layernorm(o):
    of = o.astype(jnp.float32)
    mu = jnp.mean(of, axis=-1, keepdims=True)
    var = jnp.mean(jnp.square(of - mu), axis=-1, keepdims=True)
    return ((of - mu) * lax.rsqrt(var + EPS)).astype(o.dtype)


def modulate(h, shift, scale):
    return h * (1.0 + scale) + shift


def to_heads(t, n_heads):
    b, l, _ = t.shape
    return t.reshape(b, l, n_heads, -1).transpose(0, 2, 1, 3)


def from_heads(t):
    b, h, l, d = t.shape
    return t.transpose(0, 2, 1, 3).reshape(b, l, h * d)


def grid_positions(n_rows):
    row = jnp.repeat(jnp.arange(n_rows), GRID_W).astype(jnp.float32)
    col = jnp.tile(jnp.arange(GRID_W), n_rows).astype(jnp.float32)
    return row, col


def rotary_2d(t, row_pos, col_pos):
    d = t.shape[-1]
    half = d // 2
    nf = half // 2
    inv = ROPE_BASE ** (-jnp.arange(nf, dtype=jnp.float32) / nf)

    def rot(u, p):
        ang = p[:, None] * inv[None, :]
        cos, sin = jnp.cos(ang), jnp.sin(ang)
        u1, u2 = u[..., :nf], u[..., nf:]
        return jnp.concatenate([u1 * cos - u2 * sin, u1 * sin + u2 * cos], axis=-1)

    tf = t.astype(jnp.float32)
    out = jnp.concatenate([rot(tf[..., :half], row_pos), rot(tf[..., half:], col_pos)], axis=-1)
    return out.astype(t.dtype)


def chunked_gated_linear_attention(q, k, v, log_a, s0, strict):
    b_, h_, l_, _ = q.shape
    dv = v.shape[-1]
    n_chunks = l_ // CHUNK
    f32 = jnp.float32

    def blocks(t):
        return t.astype(f32).reshape(b_, h_, n_chunks, CHUNK, t.shape[-1]).transpose(2, 0, 1, 3, 4)

    qs, ks, vs, gs = blocks(q), blocks(k), blocks(v), blocks(log_a)
    mask = jnp.tril(jnp.ones((CHUNK, CHUNK), dtype=bool), -1 if strict else 0)

    def step(s, blk):
        qc, kc, vc, gc = blk
        cum = jnp.cumsum(gc, axis=-2)
        cum_last = cum[..., -1:, :]
        q_dec = qc * jnp.exp(cum)
        k_inv = kc * jnp.exp(-cum)
        k_end = kc * jnp.exp(cum_last - cum)
        scores = jnp.where(mask, jnp.einsum('bhtd,bhsd->bhts', q_dec, k_inv), 0.0)
        o = jnp.einsum('bhts,bhsv->bhtv', scores, vc) + jnp.einsum('bhtd,bhdv->bhtv', q_dec, s)
        s_new = jnp.exp(cum_last[..., 0, :])[..., :, None] * s + jnp.einsum('bhsd,bhsv->bhdv', k_end, vc)
        return s_new, o

    s_fin, o = lax.scan(step, s0.astype(f32), (qs, ks, vs, gs))
    o = o.transpose(1, 2, 0, 3, 4).reshape(b_, h_, l_, dv)
    return o.astype(v.dtype), s_fin


def bidirectional_scan(lat_in, ctx_in):
    q, k, v, a_f, a_b = lat_in
    cq, ck, cv, ca_f, ca_b = ctx_in
    b_, h_, _, dk = q.shape
    dv = v.shape[-1]
    zero = jnp.zeros((b_, h_, dk, dv), jnp.float32)

    def flip(t):
        return jnp.flip(t, axis=2)

    c_f, s_f = chunked_gated_linear_attention(cq, ck, cv, ca_f, zero, False)
    x_f, _ = chunked_gated_linear_attention(q, k, v, a_f, s_f, False)
    c_b, s_b = chunked_gated_linear_attention(flip(cq), flip(ck), flip(cv), flip(ca_b), zero, True)
    x_b, _ = chunked_gated_linear_attention(flip(q), flip(k), flip(v), flip(a_b), s_b, True)
    return x_f + flip(x_b), c_f + flip(c_b)


def project_in(h, w_in):
    offsets = np.cumsum(np.array(IN_SPLITS))[:-1].tolist()
    return jnp.split(h @ w_in, offsets, axis=-1)


def gla_log_gate(d, up, ub):
    return jax.nn.log_sigmoid((d @ up + ub).astype(jnp.float32)) / GLA_GATE_NORM


def hybrid_mixer(h, hc, w_in, ret_decay, gate_up, gate_b, gla_norm_g, row_pos, col_pos, need_ctx):
    lat = project_in(h, w_in)
    cx = project_in(hc, w_in)
    log_gamma = jnp.log1p(-jnp.exp(ret_decay.astype(jnp.float32)))

    def retention_inputs(parts, rotate):
        q = to_heads(parts[0], N_RET_HEADS) * (RET_HEAD_DIM ** -0.5)
        k = to_heads(parts[1], N_RET_HEADS)
        if rotate:
            q = rotary_2d(q, row_pos, col_pos)
            k = rotary_2d(k, row_pos, col_pos)
        v = to_heads(parts[2], N_RET_HEADS)
        shape = q.shape[:3] + (1,)
        a_f = jnp.broadcast_to(log_gamma[0][None, :, None, None], shape)
        a_b = jnp.broadcast_to(log_gamma[1][None, :, None, None], shape)
        return q, k, v, a_f, a_b

    def gla_inputs(parts):
        q = to_heads(parts[4], N_GLA_HEADS) * (GLA_KEY_DIM ** -0.5)
        k = to_heads(parts[5], N_GLA_HEADS)
        v = to_heads(parts[6], N_GLA_HEADS)
        a_f = to_heads(gla_log_gate(parts[8], gate_up[0], gate_b[0]), N_GLA_HEADS)
        a_b = to_heads(gla_log_gate(parts[9], gate_up[1], gate_b[1]), N_GLA_HEADS)
        return q, k, v, a_f, a_b

    r_lat, r_ctx = bidirectional_scan(retention_inputs(lat, True), retention_inputs(cx, False))
    g_lat, g_ctx = bidirectional_scan(gla_inputs(lat), gla_inputs(cx))

    def merge(r, g, parts):
        ret = from_heads(head_layernorm(r)) * jax.nn.silu(parts[3])
        gla = from_heads(rmsnorm(g, gla_norm_g)) * jax.nn.silu(parts[7])
        return jnp.concatenate([ret, gla], axis=-1)

    y_lat = merge(r_lat, g_lat, lat)
    y_ctx = merge(r_ctx, g_ctx, cx) if need_ctx else None
    return y_lat, y_ctx


def squared_relu_mlp(h, w1, w2):
    return jnp.square(jax.nn.relu(h @ w1)) @ w2


def setup_inputs(seed: int = 0) -> dict:
    key = jax.random.key(seed)
    ks = jax.random.split(key, 17)
    f32 = jnp.float32

    def nrm(k, shape, s):
        return jax.random.normal(k, shape, f32) * s

    heads = jnp.arange(N_RET_HEADS, dtype=f32)
    base = jnp.stack([-(5.0 + heads) * LN2, -(5.5 + heads) * LN2], axis=0)
    ret_decay = base[None] + nrm(ks[8], (DEPTH, 2, N_RET_HEADS), 0.01)
    return {
        'x': nrm(ks[0], (BATCH, SEQ, D_MODEL), 1.0),
        'c': nrm(ks[1], (BATCH, D_MODEL), 1.0),
        'ctx': nrm(ks[2], (BATCH, CTX_LEN, D_MODEL), 1.0),
        'c_ctx': nrm(ks[3], (D_MODEL,), 1.0),
        'ada_w': nrm(ks[4], (DEPTH, D_MODEL, 6 * D_MODEL), 0.5 * D_MODEL ** -0.5),
        'ada_b': nrm(ks[5], (DEPTH, 6 * D_MODEL), 0.02),
        'norm1_g': 1.0 + nrm(ks[6], (DEPTH, D_MODEL), 0.02),
        'w_in': nrm(ks[7], (DEPTH, D_MODEL, IN_W), D_MODEL ** -0.5),
        'ret_decay': ret_decay,
        'gla_gate_up': nrm(ks[9], (DEPTH, 2, GLA_GATE_RANK, GLA_K), GLA_GATE_RANK ** -0.5),
        'gla_gate_b': nrm(ks[10], (DEPTH, 2, GLA_K), 0.1),
        'gla_norm_g': 1.0 + nrm(ks[11], (DEPTH, GLA_VAL_DIM), 0.02),
        'w_out': nrm(ks[12], (DEPTH, MIX_W, D_MODEL), MIX_W ** -0.5),
        'norm2_g': 1.0 + nrm(ks[13], (DEPTH, D_MODEL), 0.02),
        'w_mlp1': nrm(ks[14], (DEPTH, D_MODEL, D_FF), D_MODEL ** -0.5),
        'w_mlp2': nrm(ks[15], (DEPTH, D_FF, D_MODEL), D_FF ** -0.5),
        'final_g': 1.0 + nrm(ks[16], (D_MODEL,), 0.02),
    }


def reference(x, c, ctx, c_ctx, ada_w, ada_b, norm1_g, w_in, ret_decay, gla_gate_up, gla_gate_b,
              gla_norm_g, w_out, norm2_g, w_mlp1, w_mlp2, final_g):
    n_rows = x.shape[1] // GRID_W
    row_pos, col_pos = grid_positions(n_rows)
    silu_c = jax.nn.silu(c)
    silu_cc = jax.nn.silu(c_ctx)
    for layer in range(DEPTH):
        need_ctx = layer < DEPTH - 1
        mod_x = silu_c @ ada_w[layer] + ada_b[layer]
        mod_c = silu_cc @ ada_w[layer] + ada_b[layer]
        sh1, sc1, g1, sh2, sc2, g2 = [m[:, None, :] for m in jnp.split(mod_x, 6, axis=-1)]
        csh1, csc1, cg1, csh2, csc2, cg2 = jnp.split(mod_c, 6, axis=-1)

        hx = modulate(rmsnorm(x, norm1_g[layer]), sh1, sc1)
        hc = modulate(rmsnorm(ctx, norm1_g[layer]), csh1, csc1)
        mx, mc = hybrid_mixer(hx, hc, w_in[layer], ret_decay[layer], gla_gate_up[layer], gla_gate_b[layer],
                              gla_norm_g[layer], row_pos, col_pos, need_ctx)
        x = x + g1 * (mx @ w_out[layer])
        hx = modulate(rmsnorm(x, norm2_g[layer]), sh2, sc2)
        x = x + g2 * squared_relu_mlp(hx, w_mlp1[layer], w_mlp2[layer])

        if need_ctx:
            ctx = ctx + cg1 * (mc @ w_out[layer])
            hc = modulate(rmsnorm(ctx, norm2_g[layer]), csh2, csc2)
            ctx = ctx + cg2 * squared_relu_mlp(hc, w_mlp1[layer], w_mlp2[layer])
    return rmsnorm(x, final_g)
```

```python
import math
from contextlib import ExitStack

import numpy as np
import ml_dtypes

import concourse.bass as bass
import concourse.mybir as mybir
from concourse.bass_utils import run_bass_kernel_spmd

F32 = mybir.dt.float32
BF16 = mybir.dt.bfloat16
AF = mybir.ActivationFunctionType
ALU = mybir.AluOpType
AX = mybir.AxisListType

D = 1024
DEPTH = 2
IN_W = 3616
DFF = 4096
EPS = 1e-6
ENGS = ("pe", "act", "dve", "pool", "sp")
SEM_LIM = 30000
import os as _os0
SCHED_LAT = float(_os0.environ.get("K_LAT", "0.25"))
SCHED_PAD = float(_os0.environ.get("K_PAD", "0.0"))
SCHED_LATPE = float(_os0.environ.get("K_LATPE", "1.0"))
DEEP_A = int(_os0.environ.get("K_DEEPA", "1"))
SCHED_DBG = int(_os0.environ.get("K_DBG", "0"))
SCHED_PRIO = int(_os0.environ.get("K_PRIO", "2"))


class Buf:
    __slots__ = ("name", "w", "r")

    def __init__(self, name):
        self.name = name
        self.w = None
        self.r = []


class Op:
    __slots__ = ("eng", "fn", "deps", "idx", "dma", "chan", "chan_k", "signal", "sig_n", "waits", "cost", "seq",
                 "fin", "succ", "indeg", "rt", "prio", "tag")

    def __init__(self, eng, fn, dma):
        self.eng = eng
        self.fn = fn
        self.deps = set()
        self.idx = -1
        self.dma = dma
        self.chan = None
        self.chan_k = 0
        self.signal = False
        self.sig_n = 0
        self.waits = []
        self.cost = 0.1
        self.seq = 0
        self.fin = 0.0
        self.succ = []
        self.indeg = 0
        self.rt = 0.0
        self.prio = 0


class Prog:
    def __init__(self):
        self.streams = {e: [] for e in ENGS}
        self.n_chan = {"sp": 16, "pool": int(_os0.environ.get("K_NCHP", "14"))}
        self.chan_rr = {e: 0 for e in self.n_chan}
        self.chan_last = {}
        self.chan_cnt = {}

    halt = False
    defer = None

    nseq = 0
    segs = None
    cur_prio = 0

    def op(self, eng, fn, reads=(), writes=(), dma=False, extra=(), cost=0.1):
        if self.defer is not None:
            tg = None
            if SCHED_DBG:
                import sys as _s
                f = _s._getframe(1)
                tg = []
                for _ in range(5):
                    if f is None:
                        break
                    tg.append(f.f_lineno)
                    f = f.f_back
                tg = tuple(tg)
            self.defer.append((eng, fn, tuple(reads), tuple(writes), dma, tuple(extra), cost, tg))
            return None
        o = Op(eng, fn, dma)
        if self.halt:
            return o
        o.cost = cost
        if SCHED_DBG:
            import sys as _s
            f = _s._getframe(1)
            tg = []
            for _ in range(5):
                if f is None:
                    break
                tg.append(f.f_lineno)
                f = f.f_back
            o.tag = tuple(tg)
        self.nseq += 1
        o.seq = self.nseq
        o.prio = self.cur_prio if SCHED_PRIO == 1 else 0
        for b in reads:
            if b.w is not None:
                o.deps.add(b.w)
        for b in writes:
            if b.w is not None:
                o.deps.add(b.w)
            for r in b.r:
                o.deps.add(r)
        for d in extra:
            if d is not None:
                o.deps.add(d)
        o.deps.discard(o)
        o.deps.discard(None)
        if dma:
            c = (eng, self.chan_rr[eng])
            self.chan_rr[eng] = (self.chan_rr[eng] + 1) % self.n_chan[eng]
            o.chan = c
            prev = self.chan_last.get(c)
            if prev is not None:
                o.deps.add(prev)
            self.chan_last[c] = o
            self.chan_cnt[c] = self.chan_cnt.get(c, 0) + 1
            o.chan_k = self.chan_cnt[c]
        for b in reads:
            b.r.append(o)
        for b in writes:
            b.w = o
            b.r = []
        o.idx = len(self.streams[eng])
        self.streams[eng].append(o)
        return o

    def replay(self, lst, prio=0):
        assert self.defer is None
        self.cur_prio = prio
        for (eng, fn, reads, writes, dma, extra, cost, tg) in lst:
            o = self.op(eng, fn, reads, writes, dma, extra, cost)
            if tg is not None and o is not None:
                o.tag = tg
        self.cur_prio = 0

    def fence(self):
        if self.segs is None:
            self.segs = []
        self.segs.append({e: len(self.streams[e]) for e in ENGS})

    def _schedule(self, ops):
        import heapq
        inseg = set(id(o) for o in ops)
        for o in ops:
            o.succ = []
            o.indeg = 0
        for o in ops:
            for d in o.deps:
                if id(d) in inseg:
                    d.succ.append(o)
                    o.indeg += 1
        if SCHED_PRIO == 2:
            for o in sorted(ops, key=lambda x: -x.seq):
                b = 0.0
                for s_ in o.succ:
                    v = s_.fin + (SCHED_LAT if s_.eng != o.eng else 0.1)
                    if v > b:
                        b = v
                o.fin = b + (o.cost if not o.dma else o.cost)
            for o in ops:
                o.prio = -o.fin
            if SCHED_DBG and ops:
                print("critical path us:", round(max(o.fin for o in ops), 1))
                cur = max(ops, key=lambda x: x.fin)
                agg = {}
                seqs = []
                while cur is not None:
                    tg = getattr(cur, "tag", None)
                    key = (cur.eng, (tg[1] if len(tg) > 1 else tg[0]) if tg else 0, cur.dma)
                    seqs.append((cur.eng, key[1], round(cur.cost, 2)))
                    a = agg.setdefault(key, [0, 0.0])
                    a[0] += 1
                    a[1] += cur.cost
                    nxt = None
                    best = -1.0
                    for s_ in cur.succ:
                        v = s_.fin + (SCHED_LAT if s_.eng != cur.eng else 0.1)
                        if v > best:
                            best = v
                            nxt = s_
                    cur = nxt
                self.crit_agg = agg
                print("   chain sample:", seqs[len(seqs) // 2:len(seqs) // 2 + 90])
                for k, v in sorted(agg.items(), key=lambda kv: -kv[1][1])[:25]:
                    print("   crit", k, v[0], round(v[1], 1))
        fut = {e: [] for e in ENGS}
        rdy = {e: [] for e in ENGS}
        free = {e: 0.0 for e in ENGS}
        order = {e: [] for e in ENGS}
        for o in ops:
            if o.indeg == 0:
                o.rt = 0.0
                heapq.heappush(fut[o.eng], (0.0, o.seq, o))
        n_left = len(ops)
        LAT = SCHED_LAT
        PAD = SCHED_PAD
        while n_left:
            best = None
            for e in ENGS:
                if not fut[e] and not rdy[e]:
                    continue
                t = free[e]
                if rdy[e] or (fut[e] and fut[e][0][0] <= t):
                    st = t
                else:
                    st = fut[e][0][0]
                if best is None or st < best[0]:
                    best = (st, e)
            st, e = best
            while fut[e] and fut[e][0][0] <= st:
                _, sq, o = heapq.heappop(fut[e])
                heapq.heappush(rdy[e], (o.prio, sq, o))
            _, _, o = heapq.heappop(rdy[e])
            order[e].append(o)
            if o.dma:
                free[e] = st + 0.06 if e == "sp" else st + 0.8
                o.fin = st + o.cost
            else:
                free[e] = st + o.cost
                o.fin = free[e]
            n_left -= 1
            for s_ in o.succ:
                if s_.eng == "pe" and o.eng != "pe":
                    l_ = SCHED_LATPE
                elif s_.eng != o.eng or o.dma:
                    l_ = LAT
                else:
                    l_ = 0.1
                s_.rt = max(s_.rt, o.fin + l_)
                s_.indeg -= 1
                if s_.indeg == 0:
                    heapq.heappush(fut[s_.eng], (s_.rt + (PAD if s_.prio else 0.0), s_.seq, s_))
        self.sim_time = getattr(self, "sim_time", 0.0) + max(free.values())
        if SCHED_DBG:
            print("segment sim us:", round(max(free.values()), 1), {e: round(sum(o.cost for o in order[e] if not o.dma), 1) for e in ENGS})
        return order

    def reorder(self):
        segs = list(self.segs or [])
        segs.append({e: len(self.streams[e]) for e in ENGS})
        new = {e: [] for e in ENGS}
        prev = {e: 0 for e in ENGS}
        nseg = len(segs)
        for si, sg in enumerate(segs):
            ops = []
            for e in ENGS:
                ops += self.streams[e][prev[e]:sg[e]]
            for o in ops:
                o.rt = 0.0
            order = self._schedule(ops) if ops else {e: [] for e in ENGS}
            for e in ENGS:
                new[e] += order[e]
            prev = sg
            if si < nseg - 1:
                lasts = []
                for e in ENGS:
                    for o in reversed(new[e]):
                        if not o.dma:
                            lasts.append(o)
                            break
                lasts += list(self.chan_last_at(new))
                for e in ENGS:
                    f = Op(e, lambda eng: eng.nop(), False)
                    f.deps = set(lasts)
                    new[e].append(f)
        self.streams = new
        for e in ENGS:
            for i, o in enumerate(self.streams[e]):
                o.idx = i

    def chan_last_at(self, new):
        best = {}
        for e in ENGS:
            for o in new[e]:
                if o.dma:
                    if o.chan not in best or best[o.chan].chan_k < o.chan_k:
                        best[o.chan] = o
        return best.values()

    def finalize(self):
        self.reorder()
        for e in ENGS:
            known = {}
            for o in self.streams[e]:
                best = {}
                for d in o.deps:
                    if d.dma:
                        key = ("ch", d.chan)
                        v = d.chan_k
                    else:
                        if d.eng == "pe" and o.eng == "pe" and not o.dma:
                            continue
                        key = ("en", d.eng)
                        v = d.idx
                    if v > best.get(key, (-1, None))[0]:
                        best[key] = (v, d)
                for key, (v, d) in best.items():
                    if known.get(key, -1) >= v:
                        continue
                    known[key] = v
                    o.waits.append(d)
                    if not d.dma:
                        d.signal = True
        self.sig_tot = {}
        for e in ENGS:
            n = 0
            for o in self.streams[e]:
                if o.signal and not o.dma:
                    n += 1
                    o.sig_n = n
            self.sig_tot[e] = n

    def alloc_sems(self, nc, stack):
        self.esems = {}
        for e in ENGS:
            k = max(1, (self.sig_tot[e] + SEM_LIM - 1) // SEM_LIM)
            self.esems[e] = [stack.enter_context(nc.semaphore(f"s_{e}{i}")) for i in range(k)]
        self.csems = {}
        for e, n in self.n_chan.items():
            for i in range(n):
                self.csems[(e, i)] = stack.enter_context(nc.semaphore(f"c_{e}{i}"))

    def emit_stream(self, e, eng):
        for o in self.streams[e]:
            for d in o.waits:
                if d.dma:
                    eng.wait_ge(self.csems[d.chan], 16 * d.chan_k)
                else:
                    n = d.sig_n - 1
                    eng.wait_ge(self.esems[d.eng][n // SEM_LIM], (n % SEM_LIM) + 1)
            ins = o.fn(eng)
            if o.dma:
                ins.then_inc(self.csems[o.chan], 16)
            elif o.signal:
                n = o.sig_n - 1
                ins.then_inc(self.esems[o.eng][n // SEM_LIM], 1)


CF_COLS = {}


def _const_f32(nlat):
    s = np.arange(128)[:, None].astype(np.float64)
    t = np.arange(128)[None, :].astype(np.float64)
    rep4 = lambda m: np.tile(m, (1, 4))
    parts = [
        ("identf", np.eye(128)),
        ("onesf", np.ones((128, 128))),
        ("MF4", (s <= t) * 1.0),
        ("MB4", (s > t) * 1.0),
        ("dF4", np.maximum(t - s, 0)),
        ("dB4", np.maximum(s - t, 0)),
        ("tp1", np.broadcast_to(t + 1, (128, 128))),
        ("tm", np.broadcast_to(128 - t, (128, 128))),
        ("colA", np.broadcast_to(127 - s, (128, 128))),
        ("colB", np.broadcast_to(s, (128, 128))),
        ("LTf", (s <= t) * (-1.0 / 16)),
        ("LTb", (s >= t) * (-1.0 / 16)),
        ("LSf", (s > t) * (-1.0 / 16)),
        ("LSb", (s < t) * (-1.0 / 16)),
        ("neg16", np.full((128, 2), -1.0 / 16)),
    ]
    off = 0
    cols = {}
    for name, a in parts:
        cols[name] = (off, a.shape[1])
        off += a.shape[1]
    arr = np.concatenate([np.asarray(a, dtype=np.float64) for _, a in parts], axis=1).astype(np.float32)
    global CB_COLS, CB_ARR
    CB_COLS = {}
    o_ = 0
    cb_parts = []
    for name, a in parts:
        if name in ("LTf", "LTb", "LSf", "LSb", "neg16"):
            CB_COLS[name] = (o_, a.shape[1])
            o_ += a.shape[1]
            cb_parts.append(np.asarray(a, dtype=np.float32))
    CB_ARR = np.concatenate(cb_parts, axis=1).astype(ml_dtypes.bfloat16)
    tok = np.arange(nlat * 128)
    row = (tok // 64).astype(np.float64)
    col = (tok % 64).astype(np.float64)
    inv = (10000.0 ** (-np.arange(32, dtype=np.float32) / 32)).astype(np.float64)
    ang = np.concatenate([row[:, None] * inv[None, :], col[:, None] * inv[None, :]], axis=1)
    ang = ang.astype(np.float32).astype(np.float64)
    cosT = np.cos(ang).reshape(nlat, 128, 64).transpose(1, 0, 2).astype(np.float32)
    sinT = np.sin(ang).reshape(nlat, 128, 64).transpose(1, 0, 2).astype(np.float32)
    return arr, cols, np.ascontiguousarray(cosT), np.ascontiguousarray(sinT)


def build(nlat, cols, ncf, dbg=None):
    NT = nlat + 2
    nc = bass.Bass("TRN2", target_bir_lowering=False)
    dt_in = lambda name, shape, dt=F32: nc.dram_tensor(name, list(shape), dt, kind="ExternalInput").ap()
    xin = dt_in("xin", [NT * 128, D])
    cT = dt_in("cT", [128, 16])
    ada_w = dt_in("ada_w", [DEPTH, D, 6 * D])
    ada_bT = dt_in("ada_bT", [DEPTH, 128, 48])
    n1gT = dt_in("n1gT", [DEPTH, 128, 8])
    n2gT = dt_in("n2gT", [DEPTH, 128, 8])
    w_in = dt_in("w_in", [DEPTH, D, IN_W])
    ret_decay = dt_in("ret_decay", [DEPTH, 8])
    gate_up = dt_in("gate_up", [DEPTH, 2, 16, 256])
    gate_b = dt_in("gate_b", [DEPTH, 2, 256])
    gla_g = dt_in("gla_g", [DEPTH, 128])
    w_out = dt_in("w_out", [DEPTH, D, D])
    w1 = dt_in("w1", [DEPTH, D, DFF])
    w2 = dt_in("w2", [DEPTH, DFF, D])
    final_g = dt_in("final_g", [D])
    cf = dt_in("cf", [128, ncf])
    cosd = dt_in("cosd", [128, nlat, 64])
    sind = dt_in("sind", [128, nlat, 64])
    identb_d = dt_in("identb_d", [128, 128], BF16)
    cb16 = dt_in("cb16", [128, 514], BF16)
    out = nc.dram_tensor("out", [nlat * 128, D], F32, kind="ExternalOutput").ap()
    xmid = nc.dram_tensor("xmid", [NT * 128, D], F32, kind="Internal").ap()
    xnext = nc.dram_tensor("xnext", [NT * 128, D], F32, kind="Internal").ap()
    sbd = nc.dram_tensor("sbd", [NT, 128, 1024], BF16, kind="Internal").ap()
    dbg_out = {}
    if dbg:
        for name, shape in dbg.items():
            dbg_out[name] = nc.dram_tensor("dbg_" + name, list(shape), F32, kind="ExternalOutput").ap()

    P = Prog()
    LNS_R = math.log(128.0 ** -0.5)
    LNS_G = math.log(64.0 ** -0.5)

    with ExitStack() as top:
        uid = [0]

        def mk(stack):
            uid[0] += 1
            u = uid[0]

            def sb(name, shape, dt=F32):
                return stack.enter_context(nc.sbuf_tensor(f"{name}_u{u}", list(shape), dt)), Buf(name)
            return sb

        sbp = mk(top)
        identb, Bidentb = sbp("identb", [128, 128], BF16)
        MODS = [sbp(f"MOD{l}", [128, 48, 2]) for l in range(DEPTH)]
        GSS = [sbp(f"GS{l}", [128, 2, 8, 2]) for l in range(DEPTH)]
        rot = {"banks": list(range(8))}

        def phase_M(layer, ADA, sbx, after=()):
            MOD, BMOD = MODS[layer]
            GS, BGS = GSS[layer]
            ABT, BABT = sbx("ABT", [128, 48])
            N1G, BN1G = sbx("N1G", [128, 8])
            N2G, BN2G = sbx("N2G", [128, 8])
            dma("sp", ABT[:, :], ada_bT[layer], [], [BABT])
            dma("sp", N1G[:, :], n1gT[layer], [], [BN1G])
            dma("sp", N2G[:, :], n2gT[layer], [], [BN2G])
            pm = palloc() if len(rot["banks"]) == 8 else 7
            for c in range(48):
                a_t, a_b = ADA[c % len(ADA)]
                src = ada_w[layer].rearrange("(k p) n -> p k n", p=128)[:, :, c * 128:(c + 1) * 128]
                dma("sp", a_t, src, [], [a_b], extra=after)
                for k in range(8):
                    mm(pf32(pm)[:, 2 * c:2 * c + 2], a_t[:, k, :], SCt[:, k, :], k == 0, k == 7,
                       [a_b, BSC], [PB(pm)])
            for w in range(2):
                tt("dve", MOD[:, :, w], pf32(pm)[:, 0:96].rearrange("p (c w) -> p c w", w=2)[:, :, w], ABT[:, :],
                   ALU.add, [PB(pm), BABT], [BMOD])
            if len(rot["banks"]) == 8:
                pfree(pm)
            for w in range(2):
                stt(GS[:, 0, :, w], MOD[:, 8:16, w], 1.0, N1G[:, :], ALU.add, ALU.mult, [BMOD, BN1G], [BGS])
                stt(GS[:, 1, :, w], MOD[:, 32:40, w], 1.0, N2G[:, :], ALU.add, ALU.mult, [BMOD, BN2G], [BGS])
        SCt, BSC = sbp("SCt", [128, 8, 2])
        CTl, BCT = sbp("CTl", [128, 16])
        MS, BMS = sbp("MS", [128, 8])
        eps_t, Beps = sbp("eps_t", [128, 1])
        lnsr_t, Blnsr = sbp("lnsr_t", [128, 1])
        lnsg_t, Blnsg = sbp("lnsg_t", [128, 1])
        one_t, Bone = sbp("one_t", [128, 1])

        banks = []
        for i in range(8):
            t = top.enter_context(nc.psum_tensor(f"pb{i}", [128, 512], F32))
            banks.append([t, Buf(f"pb{i}"), False])
        bank_rr = [0]

        pool_rr = {"front": 0, "back": 0}
        NFRONT = int(_os0.environ.get("K_NFRONT", "5"))
        NFRONT_A = int(_os0.environ.get("K_NFRONT_A", "5"))
        pools = {"front": list(range(NFRONT)), "back": list(range(NFRONT, 8))}
        rot["cur"] = None

        def palloc():
            if rot["cur"] in pools:
                bl = pools[rot["cur"]]
                i = bl[pool_rr[rot["cur"]] % len(bl)]
                pool_rr[rot["cur"]] += 1
                assert not banks[i][2], "psum bank still open"
                banks[i][2] = True
                return i
            bl = rot["banks"]
            i = bl[bank_rr[0] % len(bl)]
            bank_rr[0] = (bank_rr[0] + 1) % len(bl)
            assert not banks[i][2], "psum bank still open"
            banks[i][2] = True
            return i

        def pfree(i):
            banks[i][2] = False

        def pf32(i):
            return banks[i][0]

        def pbf(i, shape):
            t = banks[i][0]
            return t[:, :].bitcast(BF16)

        def PB(i):
            return banks[i][1]

        def fs(ap):
            n = 1
            for d in ap.shape[1:]:
                n *= d
            return n

        def dma(eng, out_ap, in_ap, reads, writes, extra=(), **kw):
            nbytes = fs(in_ap) * in_ap.shape[0] * 4
            return P.op(eng, lambda e: e.dma_start(out=out_ap, in_=in_ap, **kw), reads=reads, writes=writes, dma=True,
                        cost=2.0 + nbytes / 150e3, extra=extra)

        def mm(out_ap, lhsT, rhs, start, stop, reads, writes):
            n = fs(rhs) / 2400.0 + 0.008 if fs(rhs) >= 256 else 0.1
            if rhs.dtype == F32:
                n = 4 * max(fs(rhs), 128) / 2400.0 + 0.15
            return P.op("pe", lambda e: e.matmul(out_ap, lhsT=lhsT, rhs=rhs, start=start, stop=stop),
                        reads=reads, writes=writes, cost=n)

        def tr(out_ap, in_ap, ident_ap, reads, writes):
            return P.op("pe", lambda e: e.transpose(out=out_ap, in_=in_ap, identity=ident_ap), reads=reads, writes=writes,
                        cost=0.1)

        def act(out_ap, in_ap, func, reads, writes, bias=None, scale=None, accum=None):
            kw = {}
            c = 0.17 + fs(in_ap) * 0.00085
            if bias is not None:
                kw["bias"] = bias
                if not isinstance(bias, float):
                    c += 0.09
            if scale is not None:
                kw["scale"] = scale
                if not isinstance(scale, float):
                    c += 0.09
            if accum is not None:
                kw["accum_out"] = accum
                c += 0.1
            return P.op("act", lambda e: e.activation(out=out_ap, in_=in_ap, func=func, **kw), reads=reads, writes=writes,
                        cost=c)

        def ecost(eng, ap):
            if eng == "pool":
                return 0.15 + fs(ap) * 0.0018
            if eng == "act":
                return 0.17 + fs(ap) * 0.00085
            return 0.09 + fs(ap) * 0.00105

        def tt(eng, out_ap, a, b, op, reads, writes):
            return P.op(eng, lambda e: e.tensor_tensor(out=out_ap, in0=a, in1=b, op=op), reads=reads, writes=writes,
                        cost=ecost(eng, a))

        def ts(eng, out_ap, a, s1, s2, op0, op1, reads, writes):
            if op1 is None:
                return P.op(eng, lambda e: e.tensor_scalar(out=out_ap, in0=a, scalar1=s1, scalar2=None, op0=op0),
                            reads=reads, writes=writes, cost=ecost(eng, a))
            return P.op(eng, lambda e: e.tensor_scalar(out=out_ap, in0=a, scalar1=s1, scalar2=s2, op0=op0, op1=op1),
                        reads=reads, writes=writes, cost=ecost(eng, a))

        def stt(out_ap, a, s, b, op0, op1, reads, writes):
            return P.op("dve", lambda e: e.scalar_tensor_tensor(out=out_ap, in0=a, scalar=s, in1=b, op0=op0, op1=op1),
                        reads=reads, writes=writes, cost=ecost("dve", a))

        def cp(eng, out_ap, in_ap, reads, writes):
            if eng == "act":
                return P.op("act", lambda e: e.copy(out=out_ap, in_=in_ap), reads=reads, writes=writes, cost=ecost(eng, in_ap))
            return P.op(eng, lambda e: e.tensor_copy(out=out_ap, in_=in_ap), reads=reads, writes=writes, cost=ecost(eng, in_ap))

        def memset(eng, ap, val, writes):
            return P.op(eng, lambda e: e.memset(ap, val), writes=writes, cost=ecost(eng, ap))

        def bcast_mid(ap2, n):
            a = [list(x) for x in ap2.ap]
            return bass.AP(ap2.tensor, ap2.offset, [a[0], [0, n]] + a[1:])

        ms_rr = [0]
        MSB = [Buf(f"ms{i}") for i in range(8)]

        def ms_slot():
            i = ms_rr[0]
            ms_rr[0] = (i + 1) % 8
            return MS[:, i:i + 1], MSB[i]

        dma("sp", identb[:, :], identb_d[:, :], [], [Bidentb])
        dma("sp", CTl[:, :], cT[:, :], [], [BCT])
        memset("dve", eps_t[:, :], EPS, [Beps])
        memset("dve", lnsr_t[:, :], LNS_R, [Blnsr])
        memset("dve", lnsg_t[:, :], LNS_G, [Blnsg])
        memset("dve", one_t[:, :], 1.0, [Bone])
        act(SCt[:, :, 0], CTl[:, 0:8], AF.Silu, [BCT], [BSC])
        act(SCt[:, :, 1], CTl[:, 8:16], AF.Silu, [BCT], [BSC])

        def rstd_from(ms_ap, Bms_list, scale_in):
            act(ms_ap, ms_ap, AF.Ln, Bms_list + [Beps], Bms_list, bias=eps_t[:, 0:1], scale=scale_in)
            act(ms_ap, ms_ap, AF.Exp, Bms_list, Bms_list, scale=-0.5)

        import os as _os
        _stop = _os.environ.get("KSTOP", "")

        class _Stop(Exception):
            pass

        def chk(tag):
            if _stop == tag:
                P.halt = True

        for layer in range(DEPTH):
          try:
              last = layer == DEPTH - 1
              xsrc = xin if layer == 0 else xnext

              with ExitStack() as ph:
                  sb = mk(ph)
                  ncfa = cols["dF4"][0]
                  ncfb = cols["LTf"][0] - ncfa
                  CF, BCF = sb("CF", [128, ncfa])
                  CB16, BCB16 = sb("CB16", [128, 514], BF16)
                  dma("sp", CB16[:, :], cb16[:, :], [], [BCB16])

                  def CB(name):
                      o, w = CB_COLS[name]
                      return CB16[:, o:o + w]

                  def C(name, lo=0, n=None):
                      o, w = cols[name]
                      n = w if n is None else n
                      if o >= ncfa + ncfb:
                          return CF[:, o - ncfb + lo:o - ncfb + lo + n]
                      if o >= ncfa:
                          return CFB[:, o - ncfa + lo:o - ncfa + lo + n]
                      return CF[:, o + lo:o + lo + n]

                  XM0 = sb("XM", [128, D])
                  XM1 = sb("XM1", [128, D])
                  XM_ = [XM0, XM1] * 3
                  CFB = XM1[0]
                  dma("sp", CF[:, 0:ncfa], cf[:, 0:ncfa], [], [BCF])
                  dma("sp", CFB[:, 0:ncfb], cf[:, ncfa:ncfa + ncfb], [], [XM1[1]])
                  BCFB = XM1[1]
                  TMP0 = sb("TMP", [128, D])
                  TMP_ = [TMP0] * 6
                  ADA = [(t_[:, :].rearrange("p (k n) -> p k n", k=8), b_) for (t_, b_) in (XM0, TMP0)]
                  MOD, BMOD = MODS[layer]
                  GS, BGS = GSS[layer]
                  if layer == 0 or _os.environ.get("KNOOVL"):
                      phase_M(layer, ADA, sb)

                  def gate_tile(dst, Bdst, m, w):
                      DG, BDG = GT_tmp
                      for j in range(8):
                          ts("dve", DG[:, :], C("identf"), MOD[:, m * 8 + j, w:w + 1], None, ALU.mult, None,
                             [BCF, BMOD], [BDG])
                          pb_ = palloc()
                          mm(pf32(pb_)[:, 0:128], C("onesf"), DG[:, :], True, True, [BCF, BDG], [PB(pb_)])
                          cp("act", dst[:, j * 128:(j + 1) * 128], pf32(pb_)[:, 0:128], [PB(pb_)], [Bdst])
                          pfree(pb_)

                  GT_tmp = sb("DG", [128, 128])
                  G1, BG1 = sb("G1", [128, D])
                  chk('M0')
                  if not last:
                      gate_tile(G1, BG1, 2, 1)

                  WIN, BWIN = sb("WIN", [128, 8, IN_W], BF16)
                  WOUT, BWOUT = sb("WOUT", [128, 8, D], BF16)
                  BWINp = {(k, c0): Buf(f"WIN{k}_{c0}") for k in range(8) for c0 in (0, 1024, 2048, 3072)}
                  for (c0, c1) in ((3072, IN_W), (2048, 3072), (0, 1024), (1024, 2048)):
                      for k in range(8):
                          dma("pool", WIN[:, k, c0:c1], w_in[layer, k * 128:(k + 1) * 128, c0:c1], [], [BWINp[(k, c0)]])
                  BWOk = [Buf(f"WO{k}") for k in range(8)]
                  for k in range(8):
                      dma("pool", WOUT[:, k, :], w_out[layer, k * 128:(k + 1) * 128, :], [], [BWOk[k]])

                  chk('W')
                  COS_ = [sb(f"COS{i}", [128, 64]) for i in range(2)]
                  SIN_ = [sb(f"SIN{i}", [128, 64]) for i in range(2)]
                  RD, BRD = sb("RD", [128, 8])
                  LG, BLG = sb("LG", [128, 8])
                  G128, BG128 = sb("G128", [128, 8])
                  dma("sp", RD[:, :], bass.AP(ret_decay.tensor, layer * 8, [[0, 128], [1, 8]]), [], [BRD])
                  act(LG[:, :], RD[:, :], AF.Exp, [BRD], [BLG])
                  act(LG[:, :], LG[:, :], AF.Ln, [BLG, Bone], [BLG], bias=one_t[:, 0:1], scale=-1.0)
                  act(G128[:, :], LG[:, :], AF.Exp, [BLG], [BG128], scale=128.0)
                  D4, BD4 = sb("D4", [128, 4, 128])
                  WF, BWF = sb("WF", [128, 4, 128], BF16)
                  WB, BWB = sb("WB", [128, 4, 128], BF16)
                  SFF, BSFF = sb("SFF", [128, 4, 128])
                  SBF, BSBF = sb("SBF", [128, 4, 128])
                  TA, BTA = sb("TA", [128, 128])
                  TB_, BTB = sb("TB_", [128, 128])
                  for h in range(4):
                      lgf = LG[:, h:h + 1]
                      lgb = LG[:, 4 + h:5 + h]
                      act(TA[:, :], C("dF4", 0, 128), AF.Exp, [BCFB, BLG, Blnsr], [BTA], bias=lnsr_t[:, 0:1], scale=lgf)
                      act(TB_[:, :], C("dB4", 0, 128), AF.Exp, [BCFB, BLG, Blnsr], [BTB], bias=lnsr_t[:, 0:1], scale=lgb)
                      tt("dve", TA[:, :], TA[:, :], C("MF4", 0, 128), ALU.mult, [BTA, BCF], [BTA])
                      tt("dve", TB_[:, :], TB_[:, :], C("MB4", 0, 128), ALU.mult, [BTB, BCF], [BTB])
                      tt("dve", D4[:, h, :], TA[:, :], TB_[:, :], ALU.add, [BTA, BTB], [BD4])
                      act(WF[:, h, :], C("tp1", 0, 128), AF.Exp, [BCFB, BLG, Blnsr], [BWF], bias=lnsr_t[:, 0:1], scale=lgf)
                      act(WB[:, h, :], C("tm", 0, 128), AF.Exp, [BCFB, BLG, Blnsr], [BWB], bias=lnsr_t[:, 0:1], scale=lgb)
                      act(SFF[:, h, :], C("colA"), AF.Exp, [BCFB, BLG], [BSFF], scale=lgf)
                      act(SBF[:, h, :], C("colB"), AF.Exp, [BCFB, BLG], [BSBF], scale=lgb)
                  UPB, BUPB = sb("UPB", [33, 4, 2, 64], BF16)
                  memset("pool", UPB[:, :, :, :], 0.0, [BUPB])
                  for dr in range(2):
                      dma("pool", UPB[dr * 16:(dr + 1) * 16, :, dr, :],
                          gate_up[layer, dr].rearrange("r (h d) -> r h d", h=4), [], [BUPB])
                      dma("pool", UPB[32:33, :, dr, :], gate_b[layer, dr:dr + 1].rearrange("o (h d) -> o h d", h=4), [], [BUPB])
                  GNB, BGNB = sb("GNB", [128, 128])
                  dma("sp", GNB[:, :], bass.AP(gla_g.tensor, layer * 128, [[0, 128], [1, 128]]), [], [BGNB])

                  chk('K')
                  RS32f, BRS32f = sb("RS32f", [128, 4, 128])
                  RS32b, BRS32b = sb("RS32b", [128, 4, 128])
                  GS32, _ = sb("GS32", [128, 4, 128])
                  BGS32f, BGS32b = Buf("GS32f"), Buf("GS32b")
                  memset("pool", RS32f[:, :, :], 0.0, [BRS32f])
                  memset("pool", RS32b[:, :, :], 0.0, [BRS32b])
                  memset("pool", GS32[:, :, :], 0.0, [BGS32f, BGS32b])
                  RSfbf = [sb(f"RSfbf{i}", [128, 4, 128], BF16) for i in range(2)]
                  RSbbf = [sb(f"RSbbf{i}", [128, 4, 128], BF16) for i in range(2)]
                  GSbf = [sb(f"GSbf{i}", [128, 4, 128], BF16)[0] for i in range(2)]
                  BGSbf_f = [Buf(f"GSbf_f{i}") for i in range(2)]
                  BGSbf_b = [Buf(f"GSbf_b{i}") for i in range(2)]

                  def dbl(name, shape, dt=F32, n=2):
                      l_ = [sb(f"{name}{i}", shape, dt) for i in range(n)]
                      return [l_[i % n] for i in range(6)]

                  X_ = dbl("X", [128, D])
                  XN_ = dbl("XN", [128, D], BF16)
                  HT_ = dbl("HT", [128, 8, 128], BF16)
                  DT_ = dbl("DT", [33, 128], BF16)
                  for i in range(2):
                      memset("pool", DT_[i][0][:, :], 1.0, [DT_[i][1]])
                  L_ = dbl("L", [128, 4, 2, 64])
                  ECUM_ = dbl("ECUM", [128, 2, 4, 64])
                  EINV_ = dbl("EINV", [128, 2, 4, 64])
                  EEND_ = dbl("EEND", [128, 2, 4, 64])
                  GA_ = dbl("GA", [128, 4], n=3)
                  QD_ = dbl("QD", [128, 4, 2, 64], BF16)
                  KI_ = dbl("KI", [128, 4, 2, 64], BF16)
                  KE_ = dbl("KE", [128, 4, 2, 64], BF16, n=3)
                  VG_ = dbl("VG", [128, 4, 128], BF16)
                  GTT_ = dbl("GTT", [128, 2, 4, 128], BF16)
                  PTf_ = dbl("PTf", [128, 4, 128], BF16)
                  PTb_ = dbl("PTb", [128, 4, 128], BF16)
                  SGg_ = dbl("SGg", [128, 4, 128], BF16)
                  GG_ = dbl("GG", [128, 4, 128], BF16)
                  SGr_ = dbl("SGr", [128, 4, 128], BF16)
                  RTA = [sb(f"RTA{i}", [128, 256]) for i in range(2)]
                  RTB = [sb(f"RTB{i}", [128, 256]) for i in range(2)]
                  QR_ = dbl("QR", [128, 4, 128], BF16)
                  KR_ = dbl("KR", [128, 4, 128], BF16)
                  RTq_ = dbl("RTq", [128, 4, 128], BF16)
                  RTqf_ = dbl("RTqf", [128, 4, 128], BF16)
                  RTqb_ = dbl("RTqb", [128, 4, 128], BF16)
                  RTk_ = dbl("RTk", [128, 4, 128], BF16)
                  VR_ = dbl("VR", [128, 4, 128], BF16)
                  VV_ = dbl("VV", [128, 4, 2, 128], BF16)
                  PTr_ = dbl("PTr", [128, 4, 128], BF16)
                  OC_ = dbl("OC", [128, 4, 128], n=1)
                  ST_ = dbl("ST", [128, 16])
                  Y_ = dbl("Y", [128, 8, 128], BF16, n=1)
                  YT_ = dbl("YT", [128, 8, 128], BF16, n=1)
                  if _os.environ.get("KDBG"):
                      print("sbuf remaining (mixer phase):", nc.sbuf_bytes_remaining)
                  Bx = [Buf(f"xtile{i}") for i in range(NT)]
                  Bxm = [Buf(f"xmid{i}") for i in range(NT)]
                  Bsbd = [Buf(f"sbd{i}") for i in range(NT)]

                  def silu_evac(dst, Bdst, pbank, Lbuf):
                      EG = Lbuf[0][:, :, :, :].rearrange("p h r d -> p (h r d)")
                      BEG = Lbuf[1]
                      act(EG, pf32(pbank)[:, :], AF.Exp, [PB(pbank)], [BEG], scale=-1.0)
                      act(EG, EG, AF.Ln, [BEG, Bone], [BEG], bias=one_t[:, 0:1], scale=1.0)
                      act(EG, EG, AF.Exp, [BEG], [BEG], scale=-1.0)
                      tt("dve", dst, pf32(pbank)[:, :], EG, ALU.mult, [PB(pbank), BEG], [Bdst])

                  X_A = [X_[0], X_[1], XM0, XM1]
                  XN_A = [XN_[0], XN_[1],
                          (Y_[0][0][:, :, :].rearrange("p j t -> p (j t)"), Y_[0][1]),
                          (YT_[0][0][:, :, :].rearrange("p j t -> p (j t)"), YT_[0][1])]
                  HT_A = [HT_[0], HT_[1]] + [(GTT_[i][0][:, :, :, :].rearrange("p a h t -> p (a h) t"), GTT_[i][1]) for i in range(2)]
                  L_A = [L_[0], L_[1]] + [(ECUM_[i][0][:, :, :, :].rearrange("p r h d -> p (r h d)").rearrange(
                      "p (h r d) -> p h r d", h=4, r=2), ECUM_[i][1]) for i in range(2)]
                  EEND_A = [EEND_[0], EEND_[1], EINV_[0], EINV_[1]]

                  def tile_proc(ti, nidx, mode):
                      par = nidx % 6
                      is_ctx = ti < 2
                      w = 1 if is_ctx else 0
                      need_q = mode == "B"
                      lists = {"front": [], "back": []}

                      nf_ = NFRONT_A if mode in ("A", "STORE") else NFRONT
                      pools["front"] = list(range(nf_))
                      pools["back"] = list(range(nf_, 8))

                      def sec(name):
                          P.defer = lists[name]
                          rot["cur"] = name

                      if mode in ("A", "STORE"):
                          sec("back")
                          store_entering(ti)
                      sec("front")
                      if mode == "STORE":
                          P.defer = None
                          rot["cur"] = None
                          return lists
                      deepA = mode == "A" and DEEP_A
                      X, BX = X_A[nidx % 4] if deepA else X_[par]
                      dma("sp", X[:, :], xsrc[ti * 128:(ti + 1) * 128, :], [Bx[ti]], [BX])
                      msa, Bm = ms_slot()
                      XN, BXN = XN_A[nidx % 4] if deepA else XN_[par]
                      act(XN[:, :], X[:, :], AF.Square, [BX], [BXN, Bm], scale=1.0 / 32, accum=msa)
                      rstd_from(msa, [Bm], 1.0)
                      ts("dve", XN[:, :], X[:, :], msa, None, ALU.mult, None, [BX, Bm], [BXN])
                      ptx = palloc()
                      txv = pbf(ptx, None)
                      for j in range(8):
                          tr(txv[:, j * 128:(j + 1) * 128], XN[:, j * 128:(j + 1) * 128], identb[:, :], [BXN, Bidentb], [PB(ptx)])
                      HT, BHT = HT_A[nidx % 4] if deepA else HT_[par]
                      for j in range(8):
                          if j % 2 == 0:
                              ts("dve", HT[:, j, :], txv[:, j * 128:(j + 1) * 128], GS[:, 0, j, w:w + 1],
                                 MOD[:, j, w:w + 1], ALU.mult, ALU.add, [PB(ptx), BGS, BMOD], [BHT])
                          else:
                              act(HT[:, j, :], txv[:, j * 128:(j + 1) * 128], AF.Identity, [PB(ptx), BGS, BMOD], [BHT],
                                  bias=MOD[:, j, w:w + 1], scale=GS[:, 0, j, w:w + 1])
                      pfree(ptx)

                      def wbufs(k, c0, c1):
                          return [BWINp[(k, p0)] for p0 in (0, 1024, 2048, 3072) if p0 < c1 and min(p0 + 1024, IN_W) > c0]

                      def proj(c0, c1, pbank, col0=0, M=None):
                          n = c1 - c0
                          for k in range(8):
                              mm(pf32(pbank)[:, col0:col0 + n], HT[:, k, :], WIN[:, k, c0:c1], k == 0, k == 7,
                                 [BHT] + wbufs(k, c0, c1), [PB(pbank)])

                      prk = palloc()
                      for k in range(8):
                          mm(pf32(prk)[0:32, 0:128], WIN[:, k, 3584:3616], HT[:, k, :], k == 0, k == 7,
                             [BHT] + wbufs(k, 3584, 3616), [PB(prk)])
                      DT, BDT = DT_[par]
                      cp("act", DT[0:32, :], pf32(prk)[0:32, 0:128], [PB(prk)], [BDT])
                      pfree(prk)
                      pz = palloc()
                      mm(pf32(pz)[:, :], DT[:, :], UPB[:, :, :, :].rearrange("p h r d -> p (h r d)"), True, True,
                         [BDT, BUPB], [PB(pz)])
                      L, BL = L_A[nidx % 4] if deepA else L_[par]
                      Lflat = L[:, :, :, :].rearrange("p h r d -> p (h r d)")
                      act(Lflat, pf32(pz)[:, :], AF.Exp, [PB(pz)], [BL], scale=-1.0)
                      pfree(pz)
                      EEND, BEEND = EEND_A[nidx % 4] if deepA else EEND_[par]
                      LBflat = EEND[:, :, :, :].rearrange("p r h d -> p (r h d)").bitcast(BF16)[:, 0:512]
                      LB = LBflat.rearrange("p (h r d) -> p h r d", h=4, r=2)
                      act(LBflat, Lflat, AF.Ln, [BL, Bone], [BEEND], bias=one_t[:, 0:1], scale=1.0)
                      pcum = palloc()
                      prem = palloc()
                      mm(pf32(pcum)[:, 0:256], CB("LTf"), LB[:, :, 0, :], True, True, [BCB16, BEEND], [PB(pcum)])
                      mm(pf32(pcum)[:, 256:512], CB("LTb"), LB[:, :, 1, :], True, True, [BCB16, BEEND], [PB(pcum)])
                      mm(pf32(prem)[:, 0:256], CB("LSf"), LB[:, :, 0, :], True, True, [BCB16, BEEND], [PB(prem)])
                      mm(pf32(prem)[:, 256:512], CB("LSb"), LB[:, :, 1, :], True, True, [BCB16, BEEND], [PB(prem)])
                      ECUM, BECUM = ECUM_[par]
                      EINV, BEINV = EINV_[par]
                      EEND, BEEND = EEND_A[nidx % 4] if deepA else EEND_[par]
                      fl = lambda t_: t_[:, :, :, :].rearrange("p r h d -> p (r h d)")
                      if need_q:
                          act(fl(ECUM), pf32(pcum)[:, :], AF.Exp, [PB(pcum), Blnsg], [BECUM], bias=lnsg_t[:, 0:1], scale=1.0)
                          act(fl(EINV), pf32(pcum)[:, :], AF.Exp, [PB(pcum)], [BEINV], scale=-1.0)
                      pfree(pcum)
                      pga = palloc()
                      for h in range(4):
                          mm(pf32(pga)[:, 2 * h:2 * h + 2], LB[:, h, :, :].rearrange("p r d -> p (r d)"), CB("neg16"), True, True,
                             [BEEND, BCB16], [PB(pga)])
                      GA, BGA = GA_[par]
                      act(GA[:, :], pf32(pga)[:, 0:8].rearrange("p (h t) -> p h t", t=2)[:, :, 0], AF.Exp, [PB(pga)], [BGA])
                      pfree(pga)
                      act(fl(EEND), pf32(prem)[:, :], AF.Exp, [PB(prem)], [BEEND])
                      pfree(prem)

                      if need_q:
                          pgg = palloc()
                          proj(3072, 3584, pgg)
                          SGg, BSGg = SGg_[par]
                          silu_evac(SGg[:, :, :].rearrange("p h d -> p (h d)"), BSGg, pgg, L_[par])
                          pfree(pgg)
                          GG, BGG = GG_[par]
                          tt("pool", GG[:, :, :], SGg[:, :, :], bcast_mid(GNB[:, :], 4), ALU.mult, [BSGg, BGNB], [BGG])
                          if mode == 'B' and ti == 0: chk('b1')

                      pg3 = palloc()
                      if need_q:
                          proj(2048, 2560, pg3)
                      else:
                          proj(2304, 2560, pg3, col0=256)
                      q3 = pf32(pg3)[:, 0:256].rearrange("p (h d) -> p h d", h=4)
                      k3 = pf32(pg3)[:, 256:512].rearrange("p (h d) -> p h d", h=4)
                      QD, BQD = QD_[par]
                      KI, BKI = KI_[par]
                      KE, BKE = KE_[par]
                      for dr in range(2):
                          if need_q:
                              tt("dve", QD[:, :, dr, :], q3, ECUM[:, dr, :, :], ALU.mult, [PB(pg3), BECUM], [BQD])
                              tt("dve", KI[:, :, dr, :], k3, EINV[:, dr, :, :], ALU.mult, [PB(pg3), BEINV], [BKI])
                          tt("dve", KE[:, :, dr, :], k3, EEND[:, dr, :, :], ALU.mult, [PB(pg3), BEEND], [BKE])
                      pfree(pg3)
                      pgv = palloc()
                      proj(2560, 3072, pgv)
                      VG, BVG = VG_[par]
                      cp("act", VG[:, :, :].rearrange("p h d -> p (h d)"), pf32(pgv)[:, :], [PB(pgv)], [BVG])
                      pfree(pgv)
                      if mode == 'B' and ti == 0: chk('b2')

                      GSb, BGf, BGb = GSbf[nidx % 2], BGSbf_f[nidx % 2], BGSbf_b[nidx % 2]
                      sec("back")
                      if need_q:
                          RSb, BRSb = RSbbf[nidx % 2]
                          dma("sp", RSb[:, :, :].rearrange("p h d -> p (h d)"), sbd[ti, :, 0:512], [Bsbd[ti]], [BRSb])
                          dma("sp", GSb[64:128, :, :].rearrange("p h d -> p (h d)"), sbd[ti, 64:128, 512:1024],
                              [Bsbd[ti]], [BGb])
                          sec("front")
                          ptg = palloc()
                          tgv = pbf(ptg, None).rearrange("p (a h t) -> p a h t", a=2, h=4)
                          for h in range(4):
                              tr(tgv[:, 0, h, :], QD[:, h, :, :].rearrange("p r d -> p (r d)"), identb[:, :],
                                 [BQD, Bidentb], [PB(ptg)])
                          for h in range(4):
                              tr(tgv[:, 1, h, :], KI[:, h, :, :].rearrange("p r d -> p (r d)"), identb[:, :],
                                 [BKI, Bidentb], [PB(ptg)])
                          GTT, BGTT = GTT_[par]
                          cp("act", GTT[:, 0, :, :], tgv[:, 0, :, :], [PB(ptg)], [BGTT])
                          cp("dve", GTT[:, 1, :, :], tgv[:, 1, :, :], [PB(ptg)], [BGTT])
                          pfree(ptg)
                          if mode == 'B' and ti == 0: chk('b3')
                          psf = palloc()
                          psb = palloc()
                          for h in range(4):
                              mm(pf32(psf)[:, h * 128:(h + 1) * 128], GTT[0:64, 1, h, :], GTT[0:64, 0, h, :], True, True,
                                 [BGTT], [PB(psf)])
                          for h in range(4):
                              mm(pf32(psb)[:, h * 128:(h + 1) * 128], GTT[64:128, 1, h, :], GTT[64:128, 0, h, :], True, True,
                                 [BGTT], [PB(psb)])
                          PTf, BPTf = PTf_[par]
                          PTb, BPTb = PTb_[par]
                          tt("dve", PTf[:, :, :], pf32(psf)[:, :].rearrange("p (h t) -> p h t", h=4), bcast_mid(C("MF4"), 4), ALU.mult,
                             [PB(psf), BCF], [BPTf])
                          tt("dve", PTb[:, :, :], pf32(psb)[:, :].rearrange("p (h t) -> p h t", h=4), bcast_mid(C("MB4"), 4), ALU.mult,
                             [PB(psb), BCF], [BPTb])
                          pfree(psf)
                          pfree(psb)
                          sec("back")
                          pog = palloc()
                          for h in range(4):
                              o_ap = pf32(pog)[:, h * 128:(h + 1) * 128]
                              mm(o_ap, PTf[:, h, :], VG[:, h, :], True, False, [BPTf, BVG], [PB(pog)])
                              mm(o_ap, PTb[:, h, :], VG[:, h, :], False, False, [BPTb, BVG], [PB(pog)])
                              mm(o_ap, GTT[:, 0, h, :], GSb[:, h, :], False, True, [BGTT, BGf, BGb], [PB(pog)])
                          ST, BST = ST_[par]
                          Y, BY = Y_[par]
                          for h in range(4):
                              act(Y[:, 4 + h, :], pf32(pog)[:, h * 128:(h + 1) * 128], AF.Square, [PB(pog)], [BY, BST],
                                  accum=ST[:, 8 + h:9 + h])
                          rstd_from(ST[:, 8:12], [BST], 1.0 / 128)
                          for h in range(4):
                              stt(Y[:, 4 + h, :], pf32(pog)[:, h * 128:(h + 1) * 128], ST[:, 8 + h:9 + h], GG[:, h, :],
                                  ALU.mult, ALU.mult, [PB(pog), BST, BGG], [BY])
                          pfree(pog)
                          if mode == 'B' and ti == 0: chk('b5')
                      pgu = palloc()
                      for h in range(4):
                          mm(pf32(pgu)[:, h * 128:(h + 1) * 128], KE[:, h, :, :].rearrange("p r d -> p (r d)"), VG[:, h, :],
                             True, True, [BKE, BVG], [PB(pgu)])
                      if mode == "A":
                          lo, hi, Bst = 64, 128, BGS32b
                      else:
                          lo, hi, Bst = 0, 64, BGS32f
                      npar = (nidx + 1) % 2
                      for h in range(4):
                          stt(GS32[lo:hi, h, :], GS32[lo:hi, h, :], GA[lo:hi, h:h + 1], pf32(pgu)[lo:hi, h * 128:(h + 1) * 128],
                              ALU.mult, ALU.add, [Bst, BGA, PB(pgu)], [Bst])
                      pfree(pgu)
                      if mode != "A":
                          cp("act", GSbf[npar][0:64, :, :], GS32[0:64, :, :], [BGS32f], [BGSbf_f[npar]])
                          if mode == 'B' and ti == 0: chk('b6')

                      sec("front")
                      if need_q:
                          pgr = palloc()
                          proj(1536, 2048, pgr)
                          SGr, BSGr = SGr_[par]
                          silu_evac(SGr[:, :, :].rearrange("p h d -> p (h d)"), BSGr, pgr, L_[par])
                          pfree(pgr)
                      QR, BQR = QR_[par]
                      KR, BKR = KR_[par]
                      COS, BCOS = COS_[nidx % 2]
                      SIN, BSIN = SIN_[nidx % 2]
                      if not is_ctx:
                          dma("sp", COS[:, :], cosd[:, ti - 2, :], [], [BCOS])
                          dma("sp", SIN[:, :], sind[:, ti - 2, :], [], [BSIN])

                      RT1, BRT1 = RTA[nidx % 2]
                      RT2, BRT2 = RTB[nidx % 2]
                      RT3, BRT3 = RT1, BRT1
                      RT4, BRT4 = RT2, BRT2

                      def rotary(pbank, dst, Bdst):
                          if is_ctx:
                              cp("act", dst[:, :, :].rearrange("p h d -> p (h d)"), pf32(pbank)[:, :], [PB(pbank)], [Bdst])
                              return
                          lt = ti - 2
                          v5 = pf32(pbank)[:, :].rearrange("p (h a b f) -> p h a b f", h=4, a=2, b=2)
                          d5 = dst[:, :, :].rearrange("p h (a b f) -> p h a b f", a=2, b=2)
                          u1 = v5[:, :, :, 0, :]
                          u2 = v5[:, :, :, 1, :]
                          cosb = bcast_mid(COS[:, :].rearrange("p (a f) -> p a f", a=2), 4)
                          sinb = bcast_mid(SIN[:, :].rearrange("p (a f) -> p a f", a=2), 4)
                          r4 = lambda t_: t_[:, :].rearrange("p (h a f) -> p h a f", h=4, a=2)
                          tt("dve", r4(RT1), u1, cosb, ALU.mult, [PB(pbank), BCOS], [BRT1])
                          tt("dve", r4(RT2), u2, sinb, ALU.mult, [PB(pbank), BSIN], [BRT2])
                          tt("pool", d5[:, :, :, 0, :], r4(RT1), r4(RT2), ALU.subtract, [BRT1, BRT2], [Bdst])
                          tt("dve", r4(RT3), u1, sinb, ALU.mult, [PB(pbank), BSIN], [BRT3])
                          tt("dve", r4(RT4), u2, cosb, ALU.mult, [PB(pbank), BCOS], [BRT4])
                          tt("pool", d5[:, :, :, 1, :], r4(RT3), r4(RT4), ALU.add, [BRT3, BRT4], [Bdst])

                      if need_q:
                          pq = palloc()
                          proj(0, 512, pq)
                          rotary(pq, QR, BQR)
                          pfree(pq)
                      pk = palloc()
                      proj(512, 1024, pk)
                      rotary(pk, KR, BKR)
                      pfree(pk)
                      pv = palloc()
                      proj(1024, 1536, pv)
                      VR, BVR = VR_[par]
                      VV, BVV = VV_[par]
                      cp("act", VR[:, :, :].rearrange("p h d -> p (h d)"), pf32(pv)[:, :], [PB(pv)], [BVR])
                      pfree(pv)
                      tt("pool", VV[:, :, 0, :], VR[:, :, :], SFF[:, :, :], ALU.mult, [BVR, BSFF], [BVV])
                      tt("pool", VV[:, :, 1, :], VR[:, :, :], SBF[:, :, :], ALU.mult, [BVR, BSBF], [BVV])
                      if mode == 'B' and ti == 0: chk('b7')
                      if need_q:
                          ptr_ = palloc()
                          trv = pbf(ptr_, None).rearrange("p (a h t) -> p a h t", a=2, h=4)
                          for h in range(4):
                              tr(trv[:, 0, h, :], QR[:, h, :], identb[:, :], [BQR, Bidentb], [PB(ptr_)])
                          for h in range(4):
                              tr(trv[:, 1, h, :], KR[:, h, :], identb[:, :], [BKR, Bidentb], [PB(ptr_)])
                          RTq, BRTq = RTq_[par]
                          RTqf, BRTqf = RTqf_[par]
                          RTqb, BRTqb = RTqb_[par]
                          RTk, BRTk = RTk_[par]
                          if mode == 'B' and ti == 0: chk('b7a')
                          cp("act", RTq[:, :, :], trv[:, 0, :, :], [PB(ptr_)], [BRTq])
                          if mode == 'B' and ti == 0: chk('b7b')
                          cp("act", RTk[:, :, :], trv[:, 1, :, :], [PB(ptr_)], [BRTk])
                          if mode == 'B' and ti == 0: chk('b7c')
                          tt("pool", RTqf[:, :, :], RTq[:, :, :], WF[:, :, :], ALU.mult, [BRTq, BWF], [BRTqf])
                          if mode == 'B' and ti == 0: chk('b7d')
                          tt("pool", RTqb[:, :, :], RTq[:, :, :], WB[:, :, :], ALU.mult, [BRTq, BWB], [BRTqb])
                          pfree(ptr_)
                          if mode == 'B' and ti == 0: chk('b8')
                          psr = palloc()
                          for h in range(4):
                              mm(pf32(psr)[:, h * 128:(h + 1) * 128], RTk[:, h, :], RTq[:, h, :], True, True,
                                 [BRTk, BRTq], [PB(psr)])
                          PTr, BPTr = PTr_[par]
                          tt("dve", PTr[:, :, :].rearrange("p h d -> p (h d)"), pf32(psr)[:, :],
                             D4[:, :, :].rearrange("p h d -> p (h d)"), ALU.mult, [PB(psr), BD4], [BPTr])
                          pfree(psr)
                      sec("back")
                      if need_q:
                          por = palloc()
                          RSf, BRSf = RSfbf[nidx % 2]
                          for h in range(4):
                              o_ap = pf32(por)[:, h * 128:(h + 1) * 128]
                              mm(o_ap, PTr[:, h, :], VR[:, h, :], True, False, [BPTr, BVR], [PB(por)])
                              mm(o_ap, RTqf[:, h, :], RSf[:, h, :], False, False, [BRTqf, BRSf], [PB(por)])
                              mm(o_ap, RTqb[:, h, :], RSb[:, h, :], False, True, [BRTqb, BRSb], [PB(por)])
                          orv = pf32(por)[:, :].rearrange("p (h d) -> p h d", h=4)
                          OC, BOC = OC_[par]
                          P.op("dve", lambda e: e.tensor_reduce(out=ST[:, 0:4], in_=orv, axis=AX.X, op=ALU.add),
                               reads=[PB(por)], writes=[BST])
                          ts("dve", ST[:, 0:4], ST[:, 0:4], -1.0 / 128, None, ALU.mult, None, [BST], [BST])
                          for h in range(4):
                              act(OC[:, h, :], pf32(por)[:, h * 128:(h + 1) * 128], AF.Square, [PB(por), BST], [BOC, BST],
                                  bias=ST[:, h:h + 1], scale=1.0, accum=ST[:, 4 + h:5 + h])
                          rstd_from(ST[:, 4:8], [BST], 1.0 / 128)
                          for h in range(4):
                              ts("dve", OC[:, h, :], pf32(por)[:, h * 128:(h + 1) * 128], ST[:, h:h + 1], ST[:, 4 + h:5 + h],
                                 ALU.add, ALU.mult, [PB(por), BST], [BOC])
                          pfree(por)
                          tt("pool", Y[:, 0:4, :], OC[:, :, :], SGr[:, :, :], ALU.mult, [BOC, BSGr], [BY])
                          if mode == 'B' and ti == 0: chk('b9')
                      pru0 = palloc()
                      pru1 = palloc()
                      for h in range(4):
                          pb_ = pru0 if h < 2 else pru1
                          mm(pf32(pb_)[:, (h % 2) * 256:(h % 2) * 256 + 256], KR[:, h, :],
                             VV[:, h, :, :].rearrange("p r d -> p (r d)"), True, True, [BKR, BVV], [PB(pb_)])
                      if mode == "A":
                          for h in range(4):
                              pb_ = pru0 if h < 2 else pru1
                              stt(RS32b[:, h, :], RS32b[:, h, :], G128[:, 4 + h:5 + h],
                                  pf32(pb_)[:, (h % 2) * 256 + 128:(h % 2) * 256 + 256], ALU.mult, ALU.add,
                                  [BRS32b, BG128, PB(pb_)], [BRS32b])
                      else:
                          for h in range(4):
                              pb_ = pru0 if h < 2 else pru1
                              stt(RS32f[:, h, :], RS32f[:, h, :], G128[:, h:h + 1],
                                  pf32(pb_)[:, (h % 2) * 256:(h % 2) * 256 + 128], ALU.mult, ALU.add,
                                  [BRS32f, BG128, PB(pb_)], [BRS32f])
                          cp("act", RSfbf[npar][0][:, :, :], RS32f[:, :, :], [BRS32f], [RSfbf[npar][1]])
                      pfree(pru0)
                      pfree(pru1)
                      if mode == 'B' and ti == 0: chk('b10')

                      if need_q:
                          pty = palloc()
                          tyv = pbf(pty, None).rearrange("p (j t) -> p j t", j=8)
                          for j in range(8):
                              tr(tyv[:, j, :], Y[:, j, :], identb[:, :], [BY, Bidentb], [PB(pty)])
                          YT, BYT = YT_[par]
                          cp("act", YT[:, 0:4, :], tyv[:, 0:4, :], [PB(pty)], [BYT])
                          cp("dve", YT[:, 4:8, :], tyv[:, 4:8, :], [PB(pty)], [BYT])
                          pfree(pty)
                          TMP, BTMP = TMP_[par]
                          XM, BXM = XM_[par]
                          dma("sp", XM[:, :], xsrc[ti * 128:(ti + 1) * 128, :], [Bx[ti]], [BXM])
                          for nb in range(2):
                              po = palloc()
                              for k in range(8):
                                  mm(pf32(po)[:, :], YT[:, k, :], WOUT[:, k, nb * 512:(nb + 1) * 512], k == 0, k == 7,
                                     [BYT, BWOk[k]], [PB(po)])
                              if is_ctx:
                                  tt("dve", TMP[:, nb * 512:(nb + 1) * 512], pf32(po)[:, :], G1[:, nb * 512:(nb + 1) * 512],
                                     ALU.mult, [PB(po), BG1], [BTMP])
                              else:
                                  tt("dve", XM[:, nb * 512:(nb + 1) * 512], pf32(po)[:, :], XM[:, nb * 512:(nb + 1) * 512],
                                     ALU.add, [PB(po), BXM], [BXM])
                              pfree(po)
                          if is_ctx:
                              tt("pool", XM[:, :], TMP[:, :], XM[:, :], ALU.add, [BTMP, BXM], [BXM])
                          dma("sp", xmid[ti * 128:(ti + 1) * 128, :], XM[:, :], [BXM], [Bxm[ti]])
                      P.defer = None
                      rot["cur"] = None
                      return lists

                  RBc, BRBc = sb("RBc", [128, 4, 128], BF16)
                  GBc, BGBc = sb("GBc", [128, 4, 128], BF16)

                  def store_entering(ti):
                      cp("act", RBc[:, :, :], RS32b[:, :, :], [BRS32b], [BRBc])
                      cp("act", GBc[64:128, :, :], GS32[64:128, :, :], [BGS32b], [BGBc])
                      dma("sp", sbd[ti, :, 0:512], RBc[:, :, :].rearrange("p h d -> p (h d)"), [BRBc], [Bsbd[ti]])
                      dma("sp", sbd[ti, 64:128, 512:1024], GBc[64:128, :, :].rearrange("p h d -> p (h d)"), [BGBc], [Bsbd[ti]])

                  seq_b = [1, 0] + list(range(NT - 1, 1, -1))
                  items = [(ti, "A") for ti in seq_b[:-1]] + [(seq_b[-1], "STORE")]
                  nA = len(items)
                  for ti in range(NT):
                      items.append((ti, "S" if (last and ti < 2) else "B"))
                  cp("act", RSfbf[nA % 2][0][:, :, :], RS32f[:, :, :], [BRS32f], [RSfbf[nA % 2][1]])
                  cp("act", GSbf[nA % 2][0:64, :, :], GS32[0:64, :, :], [BGS32f], [BGSbf_f[nA % 2]])
                  def rescale_wout():
                      gate_tile(G1, BG1, 2, 0)
                      for k in range(8):
                          tt("pool", WOUT[:, k, :], WOUT[:, k, :], G1[:, :], ALU.mult, [BWOk[k], BG1], [BWOk[k]])

                  if last:
                      rescale_wout()
                  prev_back = None
                  prev_item = None
                  for n, (ti, mode) in enumerate(items):
                      lists = tile_proc(ti, n, mode)
                      P.replay(lists["front"], 1)
                      if prev_back is not None:
                          P.replay(prev_back, 0)
                          if prev_item == (1, "B"):
                              rescale_wout()
                      prev_back = lists["back"]
                      prev_item = (ti, mode)
                  P.replay(prev_back)
                  chk('B')
                  P.fence()

              with ExitStack() as ph:
                  sb = mk(ph)
                  W1, _ = sb("W1", [128, 8, DFF], BF16)
                  W2, _ = sb("W2", [128, 32, D], BF16)
                  BW1 = [[Buf(f"W1_{k}_{q}") for q in range(4)] for k in range(8)]
                  BW2 = [Buf(f"W2_{k}") for k in range(32)]
                  for q in range(4):
                      for k in range(8):
                          dma("pool", W1[:, k, q * 1024:(q + 1) * 1024], w1[layer, k * 128:(k + 1) * 128, q * 1024:(q + 1) * 1024],
                              [], [BW1[k][q]])
                  w2_last = None
                  for k in range(32):
                      w2_last = dma("pool", W2[:, k, :], w2[layer, k * 128:(k + 1) * 128, :], [], [BW2[k]])
                  identf2, Bidf2 = sb("identf2", [128, 128])
                  onesf2, Bonf2 = sb("onesf2", [128, 128])
                  o_, w_ = cols["identf"]
                  dma("sp", identf2[:, :], cf[:, o_:o_ + 128], [], [Bidf2])
                  o_, w_ = cols["onesf"]
                  dma("sp", onesf2[:, :], cf[:, o_:o_ + 128], [], [Bonf2])
                  G2x, BG2x = sb("G2x", [128, D])
                  G2c, BG2c = sb("G2c", [128, D])
                  DG2, BDG2 = sb("DG2", [128, 128])

                  def gate_tile2(dst, Bdst, m, w):
                      for j in range(8):
                          ts("dve", DG2[:, :], identf2[:, :], MOD[:, m * 8 + j, w:w + 1], None, ALU.mult, None,
                             [Bidf2, BMOD], [BDG2])
                          pb_ = palloc()
                          mm(pf32(pb_)[:, 0:128], onesf2[:, :], DG2[:, :], True, True, [Bonf2, BDG2], [PB(pb_)])
                          cp("act", dst[:, j * 128:(j + 1) * 128], pf32(pb_)[:, 0:128], [PB(pb_)], [Bdst])
                          pfree(pb_)

                  gate_tile2(G2x, BG2x, 5, 0)
                  if not last:
                      gate_tile2(G2c, BG2c, 5, 1)
                  else:
                      dma("sp", G2c[:, :], bass.AP(final_g.tensor, 0, [[0, 128], [1, D]]), [], [BG2c])

                  if layer + 1 < DEPTH and not _os.environ.get("KNOOVL"):
                      rot["banks"] = list(range(7))
                      bank_rr[0] = 0
                      ADA2 = [sb(f"ADA2{i}", [128, 8, 128]) for i in range(2)]
                      phase_M(layer + 1, [(t_[:, :, :], b_) for (t_, b_) in ADA2], sb, after=(w2_last,))
                  if _os.environ.get("KDBG"):
                      print("sbuf remaining (mlp phase, before work bufs):", nc.sbuf_bytes_remaining)
                  NXS = 4
                  XS = [sb(f"XS{i}", [128, D]) for i in range(NXS)]
                  XN2 = [sb(f"XN2{i}", [128, D], BF16) for i in range(2)]
                  HT2 = [sb(f"HT2{i}", [128, 8, 256], BF16) for i in range(2)]
                  HID0 = sb("HID", [128, 32, 256], BF16)
                  HID = [HID0, HID0]
                  RL = [sb(f"RL{i}", [128, 256]) for i in range(2)]
                  TM20 = sb("TM2", [128, D])
                  TM2 = [TM20, TM20]
                  XO = [sb(f"XO{i}", [128, D]) for i in range(2)]
                  Bxn = [Buf(f"xnext{i}") for i in range(NT)]
                  Bxm2 = [Buf(f"xmid2_{i}") for i in range(NT)]
                  tiles = list(range(2, NT)) if last else list(range(NT))
                  groups = [tiles[i:i + 2] for i in range(0, len(tiles), 2)]
                  xs_rr = 0
                  xn_rr = 0
                  for gi, grp in enumerate(groups):
                      gp = gi % 2
                      HTg, BHTg = HT2[gp]
                      HIDg, BHIDg = HID[gp]
                      xs_of = {}
                      for jj, ti in enumerate(grp):
                          Xs, BXs = XS[xs_rr % NXS]
                          xs_rr += 1
                          xs_of[ti] = (Xs, BXs)
                          w = 1 if ti < 2 else 0
                          dma("sp", Xs[:, :], xmid[ti * 128:(ti + 1) * 128, :], [Bxm2[ti]], [BXs])
                          msa, Bm = ms_slot()
                          XNt, BXNt = XN2[xn_rr % 2]
                          xn_rr += 1
                          act(XNt[:, :], Xs[:, :], AF.Square, [BXs], [BXNt, Bm], scale=1.0 / 32, accum=msa)
                          rstd_from(msa, [Bm], 1.0)
                          ts("dve", XNt[:, :], Xs[:, :], msa, None, ALU.mult, None, [BXs, Bm], [BXNt])
                          ptx = palloc()
                          txv = pbf(ptx, None)
                          for j in range(8):
                              tr(txv[:, j * 128:(j + 1) * 128], XNt[:, j * 128:(j + 1) * 128], identb[:, :],
                                 [BXNt, Bidentb], [PB(ptx)])
                          for j in range(8):
                              if j % 2 == 0:
                                  ts("dve", HTg[:, j, jj * 128:(jj + 1) * 128], txv[:, j * 128:(j + 1) * 128],
                                     GS[:, 1, j, w:w + 1], MOD[:, 24 + j, w:w + 1], ALU.mult, ALU.add,
                                     [PB(ptx), BGS, BMOD], [BHTg])
                              else:
                                  act(HTg[:, j, jj * 128:(jj + 1) * 128], txv[:, j * 128:(j + 1) * 128], AF.Identity,
                                      [PB(ptx), BGS, BMOD], [BHTg], bias=MOD[:, 24 + j, w:w + 1], scale=GS[:, 1, j, w:w + 1])
                          pfree(ptx)
                      ng = len(grp) * 128
                      for hc in range(32):
                          ph1 = palloc()
                          for k in range(8):
                              mm(pf32(ph1)[:, 0:ng], W1[:, k, hc * 128:(hc + 1) * 128], HTg[:, k, 0:ng], k == 0, k == 7,
                                 [BW1[k][hc // 8], BHTg], [PB(ph1)])
                          RLt, BRLt = RL[hc % 2]
                          act(RLt[:, 0:ng], pf32(ph1)[:, 0:ng], AF.Relu, [PB(ph1)], [BRLt])
                          pfree(ph1)
                          tt("dve" if hc % 2 == 0 else "pool", HIDg[:, hc, 0:ng], RLt[:, 0:ng], RLt[:, 0:ng], ALU.mult,
                             [BRLt], [BHIDg])
                      for jj, ti in enumerate(grp):
                          Xs, BXs = xs_of[ti]
                          is_ctx = ti < 2
                          Gt, BGt = (G2c, BG2c) if is_ctx else (G2x, BG2x)
                          TMt, BTMt = TM2[jj % 2]
                          XOt, BXOt = XO[jj % 2]
                          for nb in range(2):
                              po = palloc()
                              for k in range(32):
                                  mm(pf32(po)[:, :], HIDg[:, k, jj * 128:(jj + 1) * 128], W2[:, k, nb * 512:(nb + 1) * 512],
                                     k == 0, k == 31, [BHIDg, BW2[k]], [PB(po)])
                              tt("dve", TMt[:, nb * 512:(nb + 1) * 512], pf32(po)[:, :], Gt[:, nb * 512:(nb + 1) * 512],
                                 ALU.mult, [PB(po), BGt], [BTMt])
                              pfree(po)
                          tt("pool", XOt[:, :], TMt[:, :], Xs[:, :], ALU.add, [BTMt, BXs], [BXOt])
                          if not last:
                              dma("sp", xnext[ti * 128:(ti + 1) * 128, :], XOt[:, :], [BXOt], [Bx[ti]] if False else [Bxn[ti]])
                          else:
                              msa, Bm = ms_slot()
                              act(TMt[:, :], XOt[:, :], AF.Square, [BXOt], [BTMt, Bm], scale=1.0 / 32, accum=msa)
                              rstd_from(msa, [Bm], 1.0)
                              stt(TMt[:, :], XOt[:, :], msa, G2c[:, :], ALU.mult, ALU.mult, [BXOt, Bm, BG2c], [BTMt])
                              dma("sp", out[(ti - 2) * 128:(ti - 1) * 128, :], TMt[:, :], [BTMt], [Bxn[ti]])
                  P.fence()
                  rot["banks"] = list(range(8))
                  bank_rr[0] = 0

          except _Stop:
            break

        P.finalize()
        global _LASTP
        _LASTP = P
        P.alloc_sems(nc, top)
        with nc.Block() as block:
            @block.tensor
            def _(e):
                P.emit_stream("pe", e)

            @block.scalar
            def _(e):
                P.emit_stream("act", e)

            @block.vector
            def _(e):
                P.emit_stream("dve", e)

            @block.gpsimd
            def _(e):
                P.emit_stream("pool", e)

            @block.sync
            def _(e):
                P.emit_stream("sp", e)
                for c, k in P.chan_cnt.items():
                    e.wait_ge(P.csems[c], 16 * k)
    return nc


_CACHE = {}


def _tT(v, n):
    return np.ascontiguousarray(np.asarray(v, np.float32).reshape(n, 128).T)


def kernel(x, c, ctx, c_ctx, ada_w, ada_b, norm1_g, w_in, ret_decay, gla_gate_up, gla_gate_b,
           gla_norm_g, w_out, norm2_g, w_mlp1, w_mlp2, final_g):
    x = np.asarray(x, np.float32)
    B, S, _ = x.shape
    nlat = S // 128
    arr, cols, cosT, sinT = _const_f32(nlat)
    key = nlat
    if key not in _CACHE:
        _CACHE[key] = build(nlat, cols, arr.shape[1])
    nc = _CACHE[key]
    f = lambda a: np.ascontiguousarray(np.asarray(a, np.float32))
    shared = {
        "ada_w": f(ada_w),
        "ada_bT": np.stack([_tT(np.asarray(ada_b)[l], 48) for l in range(DEPTH)]),
        "n1gT": np.stack([_tT(np.asarray(norm1_g)[l], 8) for l in range(DEPTH)]),
        "n2gT": np.stack([_tT(np.asarray(norm2_g)[l], 8) for l in range(DEPTH)]),
        "w_in": f(w_in),
        "ret_decay": f(ret_decay).reshape(DEPTH, 8),
        "gate_up": f(gla_gate_up),
        "gate_b": f(gla_gate_b),
        "gla_g": f(gla_norm_g),
        "w_out": f(w_out),
        "w1": f(w_mlp1),
        "w2": f(w_mlp2),
        "final_g": f(final_g),
        "cf": arr,
        "cosd": cosT,
        "sind": sinT,
        "identb_d": np.eye(128).astype(ml_dtypes.bfloat16),
        "cb16": CB_ARR,
    }
    ctx = np.asarray(ctx, np.float32)
    c = np.asarray(c, np.float32)
    c_ctx = np.asarray(c_ctx, np.float32)
    in_maps = []
    for b in range(B):
        m = dict(shared)
        m["xin"] = np.ascontiguousarray(np.concatenate([ctx[b], x[b]], axis=0))
        m["cT"] = np.ascontiguousarray(np.concatenate([_tT(c[b], 8), _tT(c_ctx, 8)], axis=1))
        in_maps.append(m)
    res = run_bass_kernel_spmd(nc, in_maps, core_ids=list(range(B)))
    return np.stack([np.asarray(r["out"], np.float32) for r in res.results], axis=0)
```

```python
import math
from contextlib import ExitStack

import numpy as np
import ml_dtypes

import concourse.bass as bass
import concourse.mybir as mybir
from concourse.bass_utils import run_bass_kernel_spmd

F32 = mybir.dt.float32
BF16 = mybir.dt.bfloat16
AF = mybir.ActivationFunctionType
ALU = mybir.AluOpType
AX = mybir.AxisListType

D = 1024
DEPTH = 2
IN_W = 3616
DFF = 4096
EPS = 1e-6
ENGS = ("pe", "act", "dve", "pool", "sp")
SEM_LIM = 30000
import os as _os0
SCHED_LAT = float(_os0.environ.get("K_LAT", "0.25"))
SCHED_PAD = float(_os0.environ.get("K_PAD", "0.0"))
SCHED_LATPE = float(_os0.environ.get("K_LATPE", "1.0"))
DEEP_A = int(_os0.environ.get("K_DEEPA", "1"))
SCHED_DBG = int(_os0.environ.get("K_DBG", "0"))
SCHED_PRIO = int(_os0.environ.get("K_PRIO", "2"))


class Buf:
    __slots__ = ("name", "w", "r")

    def __init__(self, name):
        self.name = name
        self.w = None
        self.r = []


class Op:
    __slots__ = ("eng", "fn", "deps", "idx", "dma", "chan", "chan_k", "signal", "sig_n", "waits", "cost", "seq",
                 "fin", "succ", "indeg", "rt", "prio", "tag")

    def __init__(self, eng, fn, dma):
        self.eng = eng
        self.fn = fn
        self.deps = set()
        self.idx = -1
        self.dma = dma
        self.chan = None
        self.chan_k = 0
        self.signal = False
        self.sig_n = 0
        self.waits = []
        self.cost = 0.1
        self.seq = 0
        self.fin = 0.0
        self.succ = []
        self.indeg = 0
        self.rt = 0.0
        self.prio = 0


class Prog:
    def __init__(self):
        self.streams = {e: [] for e in ENGS}
        self.n_chan = {"sp": 12, "pool": int(_os0.environ.get("K_NCHP", "20"))}
        self.chan_rr = {e: 0 for e in self.n_chan}
        self.chan_last = {}
        self.chan_cnt = {}

    halt = False
    defer = None

    nseq = 0
    segs = None
    cur_prio = 0

    def op(self, eng, fn, reads=(), writes=(), dma=False, extra=(), cost=0.1):
        if self.defer is not None:
            tg = None
            if SCHED_DBG:
                import sys as _s
                f = _s._getframe(1)
                tg = []
                for _ in range(5):
                    if f is None:
                        break
                    tg.append(f.f_lineno)
                    f = f.f_back
                tg = tuple(tg)
            self.defer.append((eng, fn, tuple(reads), tuple(writes), dma, tuple(extra), cost, tg))
            return None
        o = Op(eng, fn, dma)
        if self.halt:
            return o
        o.cost = cost
        if SCHED_DBG:
            import sys as _s
            f = _s._getframe(1)
            tg = []
            for _ in range(5):
                if f is None:
                    break
                tg.append(f.f_lineno)
                f = f.f_back
            o.tag = tuple(tg)
        self.nseq += 1
        o.seq = self.nseq
        o.prio = self.cur_prio if SCHED_PRIO == 1 else 0
        for b in reads:
            if b.w is not None:
                o.deps.add(b.w)
        for b in writes:
            if b.w is not None:
                o.deps.add(b.w)
            for r in b.r:
                o.deps.add(r)
        for d in extra:
            if d is not None:
                o.deps.add(d)
        o.deps.discard(o)
        o.deps.discard(None)
        if dma:
            c = (eng, self.chan_rr[eng])
            self.chan_rr[eng] = (self.chan_rr[eng] + 1) % self.n_chan[eng]
            o.chan = c
            prev = self.chan_last.get(c)
            if prev is not None:
                o.deps.add(prev)
            self.chan_last[c] = o
            self.chan_cnt[c] = self.chan_cnt.get(c, 0) + 1
            o.chan_k = self.chan_cnt[c]
        for b in reads:
            b.r.append(o)
        for b in writes:
            b.w = o
            b.r = []
        o.idx = len(self.streams[eng])
        self.streams[eng].append(o)
        return o

    def replay(self, lst, prio=0):
        assert self.defer is None
        self.cur_prio = prio
        for (eng, fn, reads, writes, dma, extra, cost, tg) in lst:
            o = self.op(eng, fn, reads, writes, dma, extra, cost)
            if tg is not None and o is not None:
                o.tag = tg
        self.cur_prio = 0

    def fence(self):
        if self.segs is None:
            self.segs = []
        self.segs.append({e: len(self.streams[e]) for e in ENGS})

    def _schedule(self, ops):
        import heapq
        inseg = set(id(o) for o in ops)
        for o in ops:
            o.succ = []
            o.indeg = 0
        for o in ops:
            for d in o.deps:
                if id(d) in inseg:
                    d.succ.append(o)
                    o.indeg += 1
        if SCHED_PRIO == 2:
            for o in sorted(ops, key=lambda x: -x.seq):
                b = 0.0
                for s_ in o.succ:
                    v = s_.fin + (SCHED_LAT if s_.eng != o.eng else 0.1)
                    if v > b:
                        b = v
                o.fin = b + (o.cost if not o.dma else o.cost)
            for o in ops:
                o.prio = -o.fin
            if SCHED_DBG and ops:
                print("critical path us:", round(max(o.fin for o in ops), 1))
                cur = max(ops, key=lambda x: x.fin)
                agg = {}
                seqs = []
                while cur is not None:
                    tg = getattr(cur, "tag", None)
                    key = (cur.eng, (tg[1] if len(tg) > 1 else tg[0]) if tg else 0, cur.dma)
                    seqs.append((cur.eng, key[1], round(cur.cost, 2)))
                    a = agg.setdefault(key, [0, 0.0])
                    a[0] += 1
                    a[1] += cur.cost
                    nxt = None
                    best = -1.0
                    for s_ in cur.succ:
                        v = s_.fin + (SCHED_LAT if s_.eng != cur.eng else 0.1)
                        if v > best:
                            best = v
                            nxt = s_
                    cur = nxt
                self.crit_agg = agg
                print("   chain sample:", seqs[len(seqs) // 2:len(seqs) // 2 + 90])
                for k, v in sorted(agg.items(), key=lambda kv: -kv[1][1])[:25]:
                    print("   crit", k, v[0], round(v[1], 1))
        fut = {e: [] for e in ENGS}
        rdy = {e: [] for e in ENGS}
        free = {e: 0.0 for e in ENGS}
        order = {e: [] for e in ENGS}
        for o in ops:
            if o.indeg == 0:
                o.rt = 0.0
                heapq.heappush(fut[o.eng], (0.0, o.seq, o))
        n_left = len(ops)
        LAT = SCHED_LAT
        PAD = SCHED_PAD
        while n_left:
            best = None
            for e in ENGS:
                if not fut[e] and not rdy[e]:
                    continue
                t = free[e]
                if rdy[e] or (fut[e] and fut[e][0][0] <= t):
                    st = t
                else:
                    st = fut[e][0][0]
                if best is None or st < best[0]:
                    best = (st, e)
            st, e = best
            while fut[e] and fut[e][0][0] <= st:
                _, sq, o = heapq.heappop(fut[e])
                heapq.heappush(rdy[e], (o.prio, sq, o))
            _, _, o = heapq.heappop(rdy[e])
            order[e].append(o)
            if o.dma:
                free[e] = st + 0.06 if e == "sp" else st + 0.8
                o.fin = st + o.cost
            else:
                free[e] = st + o.cost
                o.fin = free[e]
            n_left -= 1
            for s_ in o.succ:
                if s_.eng == "pe" and o.eng != "pe":
                    l_ = SCHED_LATPE
                elif s_.eng != o.eng or o.dma:
                    l_ = LAT
                else:
                    l_ = 0.1
                s_.rt = max(s_.rt, o.fin + l_)
                s_.indeg -= 1
                if s_.indeg == 0:
                    heapq.heappush(fut[s_.eng], (s_.rt + (PAD if s_.prio else 0.0), s_.seq, s_))
        self.sim_time = getattr(self, "sim_time", 0.0) + max(free.values())
        if SCHED_DBG:
            print("segment sim us:", round(max(free.values()), 1), {e: round(sum(o.cost for o in order[e] if not o.dma), 1) for e in ENGS})
        return order

    def reorder(self):
        segs = list(self.segs or [])
        segs.append({e: len(self.streams[e]) for e in ENGS})
        new = {e: [] for e in ENGS}
        prev = {e: 0 for e in ENGS}
        nseg = len(segs)
        for si, sg in enumerate(segs):
            ops = []
            for e in ENGS:
                ops += self.streams[e][prev[e]:sg[e]]
            for o in ops:
                o.rt = 0.0
            order = self._schedule(ops) if ops else {e: [] for e in ENGS}
            for e in ENGS:
                new[e] += order[e]
            prev = sg
            if si < nseg - 1:
                lasts = []
                for e in ENGS:
                    for o in reversed(new[e]):
                        if not o.dma:
                            lasts.append(o)
                            break
                lasts += list(self.chan_last_at(new))
                for e in ENGS:
                    f = Op(e, lambda eng: eng.nop(), False)
                    f.deps = set(lasts)
                    new[e].append(f)
        self.streams = new
        for e in ENGS:
            for i, o in enumerate(self.streams[e]):
                o.idx = i

    def chan_last_at(self, new):
        best = {}
        for e in ENGS:
            for o in new[e]:
                if o.dma:
                    if o.chan not in best or best[o.chan].chan_k < o.chan_k:
                        best[o.chan] = o
        return best.values()

    def finalize(self):
        self.reorder()
        for e in ENGS:
            known = {}
            for o in self.streams[e]:
                best = {}
                for d in o.deps:
                    if d.dma:
                        key = ("ch", d.chan)
                        v = d.chan_k
                    else:
                        if d.eng == "pe" and o.eng == "pe" and not o.dma:
                            continue
                        key = ("en", d.eng)
                        v = d.idx
                    if v > best.get(key, (-1, None))[0]:
                        best[key] = (v, d)
                for key, (v, d) in best.items():
                    if known.get(key, -1) >= v:
                        continue
                    known[key] = v
                    o.waits.append(d)
                    if not d.dma:
                        d.signal = True
        self.sig_tot = {}
        for e in ENGS:
            n = 0
            for o in self.streams[e]:
                if o.signal and not o.dma:
                    n += 1
                    o.sig_n = n
            self.sig_tot[e] = n

    def alloc_sems(self, nc, stack):
        self.esems = {}
        for e in ENGS:
            k = max(1, (self.sig_tot[e] + SEM_LIM - 1) // SEM_LIM)
            self.esems[e] = [stack.enter_context(nc.semaphore(f"s_{e}{i}")) for i in range(k)]
        self.csems = {}
        for e, n in self.n_chan.items():
            for i in range(n):
                self.csems[(e, i)] = stack.enter_context(nc.semaphore(f"c_{e}{i}"))

    def emit_stream(self, e, eng):
        for o in self.streams[e]:
            for d in o.waits:
                if d.dma:
                    eng.wait_ge(self.csems[d.chan], 16 * d.chan_k)
                else:
                    n = d.sig_n - 1
                    eng.wait_ge(self.esems[d.eng][n // SEM_LIM], (n % SEM_LIM) + 1)
            ins = o.fn(eng)
            if o.dma:
                ins.then_inc(self.csems[o.chan], 16)
            elif o.signal:
                n = o.sig_n - 1
                ins.then_inc(self.esems[o.eng][n // SEM_LIM], 1)


CF_COLS = {}


def _const_f32(nlat):
    s = np.arange(128)[:, None].astype(np.float64)
    t = np.arange(128)[None, :].astype(np.float64)
    rep4 = lambda m: np.tile(m, (1, 4))
    parts = [
        ("identf", np.eye(128)),
        ("onesf", np.ones((128, 128))),
        ("MF4", (s <= t) * 1.0),
        ("MB4", (s > t) * 1.0),
        ("dF4", np.maximum(t - s, 0)),
        ("dB4", np.maximum(s - t, 0)),
        ("tp1", np.broadcast_to(t + 1, (128, 128))),
        ("tm", np.broadcast_to(128 - t, (128, 128))),
        ("colA", np.broadcast_to(127 - s, (128, 128))),
        ("colB", np.broadcast_to(s, (128, 128))),
        ("LTf", (s <= t) * (-1.0 / 16)),
        ("LTb", (s >= t) * (-1.0 / 16)),
        ("LSf", (s > t) * (-1.0 / 16)),
        ("LSb", (s < t) * (-1.0 / 16)),
        ("neg16", np.full((128, 2), -1.0 / 16)),
    ]
    off = 0
    cols = {}
    for name, a in parts:
        cols[name] = (off, a.shape[1])
        off += a.shape[1]
    arr = np.concatenate([np.asarray(a, dtype=np.float64) for _, a in parts], axis=1).astype(np.float32)
    global CB_COLS, CB_ARR
    CB_COLS = {}
    o_ = 0
    cb_parts = []
    for name, a in parts:
        if name in ("LTf", "LTb", "LSf", "LSb", "neg16"):
            CB_COLS[name] = (o_, a.shape[1])
            o_ += a.shape[1]
            cb_parts.append(np.asarray(a, dtype=np.float32))
    CB_ARR = np.concatenate(cb_parts, axis=1).astype(ml_dtypes.bfloat16)
    tok = np.arange(nlat * 128)
    row = (tok // 64).astype(np.float64)
    col = (tok % 64).astype(np.float64)
    inv = (10000.0 ** (-np.arange(32, dtype=np.float32) / 32)).astype(np.float64)
    ang = np.concatenate([row[:, None] * inv[None, :], col[:, None] * inv[None, :]], axis=1)
    ang = ang.astype(np.float32).astype(np.float64)
    cosT = np.cos(ang).reshape(nlat, 128, 64).transpose(1, 0, 2).astype(np.float32)
    sinT = np.sin(ang).reshape(nlat, 128, 64).transpose(1, 0, 2).astype(np.float32)
    return arr, cols, np.ascontiguousarray(cosT), np.ascontiguousarray(sinT)


def build(nlat, cols, ncf, dbg=None):
    NT = nlat + 2
    nc = bass.Bass("TRN2", target_bir_lowering=False)
    dt_in = lambda name, shape, dt=F32: nc.dram_tensor(name, list(shape), dt, kind="ExternalInput").ap()
    xin = dt_in("xin", [NT * 128, D])
    cT = dt_in("cT", [128, 16])
    ada_w = dt_in("ada_w", [DEPTH, D, 6 * D])
    ada_bT = dt_in("ada_bT", [DEPTH, 128, 48])
    n1gT = dt_in("n1gT", [DEPTH, 128, 8])
    n2gT = dt_in("n2gT", [DEPTH, 128, 8])
    w_in = dt_in("w_in", [DEPTH, D, IN_W])
    ret_decay = dt_in("ret_decay", [DEPTH, 8])
    gate_up = dt_in("gate_up", [DEPTH, 2, 16, 256])
    gate_b = dt_in("gate_b", [DEPTH, 2, 256])
    gla_g = dt_in("gla_g", [DEPTH, 128])
    w_out = dt_in("w_out", [DEPTH, D, D])
    w1 = dt_in("w1", [DEPTH, D, DFF])
    w2 = dt_in("w2", [DEPTH, DFF, D])
    final_g = dt_in("final_g", [D])
    cf = dt_in("cf", [128, ncf])
    cosd = dt_in("cosd", [128, nlat, 64])
    sind = dt_in("sind", [128, nlat, 64])
    identb_d = dt_in("identb_d", [128, 128], BF16)
    cb16 = dt_in("cb16", [128, 514], BF16)
    out = nc.dram_tensor("out", [nlat * 128, D], F32, kind="ExternalOutput").ap()
    xmid = nc.dram_tensor("xmid", [NT * 128, D], F32, kind="Internal").ap()
    xnext = nc.dram_tensor("xnext", [NT * 128, D], F32, kind="Internal").ap()
    sbd = nc.dram_tensor("sbd", [NT, 128, 1024], BF16, kind="Internal").ap()
    dbg_out = {}
    if dbg:
        for name, shape in dbg.items():
            dbg_out[name] = nc.dram_tensor("dbg_" + name, list(shape), F32, kind="ExternalOutput").ap()

    P = Prog()
    LNS_R = math.log(128.0 ** -0.5)
    LNS_G = math.log(64.0 ** -0.5)

    with ExitStack() as top:
        uid = [0]

        def mk(stack):
            uid[0] += 1
            u = uid[0]

            def sb(name, shape, dt=F32):
                return stack.enter_context(nc.sbuf_tensor(f"{name}_u{u}", list(shape), dt)), Buf(name)
            return sb

        sbp = mk(top)
        identb, Bidentb = sbp("identb", [128, 128], BF16)
        MODS = [sbp(f"MOD{l}", [128, 48, 2]) for l in range(DEPTH)]
        GSS = [sbp(f"GS{l}", [128, 2, 8, 2]) for l in range(DEPTH)]
        rot = {"banks": list(range(8))}

        def phase_M(layer, ADA, sbx, after=()):
            MOD, BMOD = MODS[layer]
            GS, BGS = GSS[layer]
            ABT, BABT = sbx("ABT", [128, 48])
            N1G, BN1G = sbx("N1G", [128, 8])
            N2G, BN2G = sbx("N2G", [128, 8])
            dma("sp", ABT[:, :], ada_bT[layer], [], [BABT])
            dma("sp", N1G[:, :], n1gT[layer], [], [BN1G])
            dma("sp", N2G[:, :], n2gT[layer], [], [BN2G])
            pm = palloc() if len(rot["banks"]) == 8 else 7
            for c in range(48):
                a_t, a_b = ADA[c % len(ADA)]
                src = ada_w[layer].rearrange("(k p) n -> p k n", p=128)[:, :, c * 128:(c + 1) * 128]
                dma("sp", a_t, src, [], [a_b], extra=after)
                for k in range(8):
                    mm(pf32(pm)[:, 2 * c:2 * c + 2], a_t[:, k, :], SCt[:, k, :], k == 0, k == 7,
                       [a_b, BSC], [PB(pm)])
            for w in range(2):
                tt("dve", MOD[:, :, w], pf32(pm)[:, 0:96].rearrange("p (c w) -> p c w", w=2)[:, :, w], ABT[:, :],
                   ALU.add, [PB(pm), BABT], [BMOD])
            if len(rot["banks"]) == 8:
                pfree(pm)
            for w in range(2):
                stt(GS[:, 0, :, w], MOD[:, 8:16, w], 1.0, N1G[:, :], ALU.add, ALU.mult, [BMOD, BN1G], [BGS])
                stt(GS[:, 1, :, w], MOD[:, 32:40, w], 1.0, N2G[:, :], ALU.add, ALU.mult, [BMOD, BN2G], [BGS])
        SCt, BSC = sbp("SCt", [128, 8, 2])
        CTl, BCT = sbp("CTl", [128, 16])
        MS, BMS = sbp("MS", [128, 8])
        eps_t, Beps = sbp("eps_t", [128, 1])
        lnsr_t, Blnsr = sbp("lnsr_t", [128, 1])
        lnsg_t, Blnsg = sbp("lnsg_t", [128, 1])
        one_t, Bone = sbp("one_t", [128, 1])

        banks = []
        for i in range(8):
            t = top.enter_context(nc.psum_tensor(f"pb{i}", [128, 512], F32))
            banks.append([t, Buf(f"pb{i}"), False])
        bank_rr = [0]

        pool_rr = {"front": 0, "back": 0}
        NFRONT = int(_os0.environ.get("K_NFRONT", "5"))
        NFRONT_A = int(_os0.environ.get("K_NFRONT_A", "5"))
        pools = {"front": list(range(NFRONT)), "back": list(range(NFRONT, 8))}
        rot["cur"] = None

        def palloc():
            if rot["cur"] in pools:
                bl = pools[rot["cur"]]
                i = bl[pool_rr[rot["cur"]] % len(bl)]
                pool_rr[rot["cur"]] += 1
                assert not banks[i][2], "psum bank still open"
                banks[i][2] = True
                return i
            bl = rot["banks"]
            i = bl[bank_rr[0] % len(bl)]
            bank_rr[0] = (bank_rr[0] + 1) % len(bl)
            assert not banks[i][2], "psum bank still open"
            banks[i][2] = True
            return i

        def pfree(i):
            banks[i][2] = False

        def pf32(i):
            return banks[i][0]

        def pbf(i, shape):
            t = banks[i][0]
            return t[:, :].bitcast(BF16)

        def PB(i):
            return banks[i][1]

        def fs(ap):
            n = 1
            for d in ap.shape[1:]:
                n *= d
            return n

        def dma(eng, out_ap, in_ap, reads, writes, extra=(), **kw):
            nbytes = fs(in_ap) * in_ap.shape[0] * 4
            return P.op(eng, lambda e: e.dma_start(out=out_ap, in_=in_ap, **kw), reads=reads, writes=writes, dma=True,
                        cost=2.0 + nbytes / 150e3, extra=extra)

        def mm(out_ap, lhsT, rhs, start, stop, reads, writes):
            n = fs(rhs) / 2400.0 + 0.008 if fs(rhs) >= 256 else 0.1
            if rhs.dtype == F32:
                n = 4 * max(fs(rhs), 128) / 2400.0 + 0.15
            return P.op("pe", lambda e: e.matmul(out_ap, lhsT=lhsT, rhs=rhs, start=start, stop=stop),
                        reads=reads, writes=writes, cost=n)

        def tr(out_ap, in_ap, ident_ap, reads, writes):
            return P.op("pe", lambda e: e.transpose(out=out_ap, in_=in_ap, identity=ident_ap), reads=reads, writes=writes,
                        cost=0.1)

        def act(out_ap, in_ap, func, reads, writes, bias=None, scale=None, accum=None):
            kw = {}
            c = 0.17 + fs(in_ap) * 0.00085
            if bias is not None:
                kw["bias"] = bias
                if not isinstance(bias, float):
                    c += 0.09
            if scale is not None:
                kw["scale"] = scale
                if not isinstance(scale, float):
                    c += 0.09
            if accum is not None:
                kw["accum_out"] = accum
                c += 0.1
            return P.op("act", lambda e: e.activation(out=out_ap, in_=in_ap, func=func, **kw), reads=reads, writes=writes,
                        cost=c)

        def ecost(eng, ap):
            if eng == "pool":
                return 0.15 + fs(ap) * 0.0018
            if eng == "act":
                return 0.17 + fs(ap) * 0.00085
            return 0.09 + fs(ap) * 0.00105

        def tt(eng, out_ap, a, b, op, reads, writes):
            return P.op(eng, lambda e: e.tensor_tensor(out=out_ap, in0=a, in1=b, op=op), reads=reads, writes=writes,
                        cost=ecost(eng, a))

        def ts(eng, out_ap, a, s1, s2, op0, op1, reads, writes):
            if op1 is None:
                return P.op(eng, lambda e: e.tensor_scalar(out=out_ap, in0=a, scalar1=s1, scalar2=None, op0=op0),
                            reads=reads, writes=writes, cost=ecost(eng, a))
            return P.op(eng, lambda e: e.tensor_scalar(out=out_ap, in0=a, scalar1=s1, scalar2=s2, op0=op0, op1=op1),
                        reads=reads, writes=writes, cost=ecost(eng, a))

        def stt(out_ap, a, s, b, op0, op1, reads, writes):
            return P.op("dve", lambda e: e.scalar_tensor_tensor(out=out_ap, in0=a, scalar=s, in1=b, op0=op0, op1=op1),
                        reads=reads, writes=writes, cost=ecost("dve", a))

        def cp(eng, out_ap, in_ap, reads, writes):
            if eng == "act":
                return P.op("act", lambda e: e.copy(out=out_ap, in_=in_ap), reads=reads, writes=writes, cost=ecost(eng, in_ap))
            return P.op(eng, lambda e: e.tensor_copy(out=out_ap, in_=in_ap), reads=reads, writes=writes, cost=ecost(eng, in_ap))

        def memset(eng, ap, val, writes):
            return P.op(eng, lambda e: e.memset(ap, val), writes=writes, cost=ecost(eng, ap))

        def bcast_mid(ap2, n):
            a = [list(x) for x in ap2.ap]
            return bass.AP(ap2.tensor, ap2.offset, [a[0], [0, n]] + a[1:])

        ms_rr = [0]
        MSB = [Buf(f"ms{i}") for i in range(8)]

        def ms_slot():
            i = ms_rr[0]
            ms_rr[0] = (i + 1) % 8
            return MS[:, i:i + 1], MSB[i]

        dma("sp", identb[:, :], identb_d[:, :], [], [Bidentb])
        dma("sp", CTl[:, :], cT[:, :], [], [BCT])
        memset("dve", eps_t[:, :], EPS, [Beps])
        memset("dve", lnsr_t[:, :], LNS_R, [Blnsr])
        memset("dve", lnsg_t[:, :], LNS_G, [Blnsg])
        memset("dve", one_t[:, :], 1.0, [Bone])
        act(SCt[:, :, 0], CTl[:, 0:8], AF.Silu, [BCT], [BSC])
        act(SCt[:, :, 1], CTl[:, 8:16], AF.Silu, [BCT], [BSC])

        def rstd_from(ms_ap, Bms_list, scale_in):
            act(ms_ap, ms_ap, AF.Ln, Bms_list + [Beps], Bms_list, bias=eps_t[:, 0:1], scale=scale_in)
            act(ms_ap, ms_ap, AF.Exp, Bms_list, Bms_list, scale=-0.5)

        import os as _os
        _stop = _os.environ.get("KSTOP", "")

        class _Stop(Exception):
            pass

        def chk(tag):
            if _stop == tag:
                P.halt = True

        for layer in range(DEPTH):
          try:
              last = layer == DEPTH - 1
              xsrc = xin if layer == 0 else xnext

              with ExitStack() as ph:
                  sb = mk(ph)
                  ncfa = cols["dF4"][0]
                  ncfb = cols["LTf"][0] - ncfa
                  CF, BCF = sb("CF", [128, ncfa])
                  CB16, BCB16 = sb("CB16", [128, 514], BF16)
                  dma("sp", CB16[:, :], cb16[:, :], [], [BCB16])

                  def CB(name):
                      o, w = CB_COLS[name]
                      return CB16[:, o:o + w]

                  def C(name, lo=0, n=None):
                      o, w = cols[name]
                      n = w if n is None else n
                      if o >= ncfa + ncfb:
                          return CF[:, o - ncfb + lo:o - ncfb + lo + n]
                      if o >= ncfa:
                          return CFB[:, o - ncfa + lo:o - ncfa + lo + n]
                      return CF[:, o + lo:o + lo + n]

                  XM0 = sb("XM", [128, D])
                  XM1 = sb("XM1", [128, D])
                  XM_ = [XM0, XM1] * 3
                  CFB = XM1[0]
                  dma("sp", CF[:, 0:ncfa], cf[:, 0:ncfa], [], [BCF])
                  dma("sp", CFB[:, 0:ncfb], cf[:, ncfa:ncfa + ncfb], [], [XM1[1]])
                  BCFB = XM1[1]
                  TMP0 = sb("TMP", [128, D])
                  TMP_ = [TMP0] * 6
                  ADA = [(t_[:, :].rearrange("p (k n) -> p k n", k=8), b_) for (t_, b_) in (XM0, TMP0)]
                  MOD, BMOD = MODS[layer]
                  GS, BGS = GSS[layer]
                  if layer == 0 or _os.environ.get("KNOOVL"):
                      phase_M(layer, ADA, sb)

                  def gate_tile(dst, Bdst, m, w):
                      DG, BDG = GT_tmp
                      for j in range(8):
                          ts("dve", DG[:, :], C("identf"), MOD[:, m * 8 + j, w:w + 1], None, ALU.mult, None,
                             [BCF, BMOD], [BDG])
                          pb_ = palloc()
                          mm(pf32(pb_)[:, 0:128], C("onesf"), DG[:, :], True, True, [BCF, BDG], [PB(pb_)])
                          cp("act", dst[:, j * 128:(j + 1) * 128], pf32(pb_)[:, 0:128], [PB(pb_)], [Bdst])
                          pfree(pb_)

                  GT_tmp = sb("DG", [128, 128])
                  G1, BG1 = sb("G1", [128, D])
                  chk('M0')
                  if not last:
                      gate_tile(G1, BG1, 2, 1)

                  WIN, BWIN = sb("WIN", [128, 8, IN_W], BF16)
                  WOUT, BWOUT = sb("WOUT", [128, 8, D], BF16)
                  BWINp = {(k, c0): Buf(f"WIN{k}_{c0}") for k in range(8) for c0 in (0, 1024, 2048, 3072)}
                  for (c0, c1) in ((3072, IN_W), (2048, 3072), (0, 1024), (1024, 2048)):
                      for k in range(8):
                          dma("pool", WIN[:, k, c0:c1], w_in[layer, k * 128:(k + 1) * 128, c0:c1], [], [BWINp[(k, c0)]])
                  BWOk = [Buf(f"WO{k}") for k in range(8)]
                  for k in range(8):
                      dma("pool", WOUT[:, k, :], w_out[layer, k * 128:(k + 1) * 128, :], [], [BWOk[k]])

                  chk('W')
                  COS_ = [sb(f"COS{i}", [128, 64]) for i in range(2)]
                  SIN_ = [sb(f"SIN{i}", [128, 64]) for i in range(2)]
                  RD, BRD = sb("RD", [128, 8])
                  LG, BLG = sb("LG", [128, 8])
                  G128, BG128 = sb("G128", [128, 8])
                  dma("sp", RD[:, :], bass.AP(ret_decay.tensor, layer * 8, [[0, 128], [1, 8]]), [], [BRD])
                  act(LG[:, :], RD[:, :], AF.Exp, [BRD], [BLG])
                  act(LG[:, :], LG[:, :], AF.Ln, [BLG, Bone], [BLG], bias=one_t[:, 0:1], scale=-1.0)
                  act(G128[:, :], LG[:, :], AF.Exp, [BLG], [BG128], scale=128.0)
                  D4, BD4 = sb("D4", [128, 4, 128])
                  WF, BWF = sb("WF", [128, 4, 128], BF16)
                  WB, BWB = sb("WB", [128, 4, 128], BF16)
                  SFF, BSFF = sb("SFF", [128, 4, 128])
                  SBF, BSBF = sb("SBF", [128, 4, 128])
                  TA, BTA = sb("TA", [128, 128])
                  TB_, BTB = sb("TB_", [128, 128])
                  for h in range(4):
                      lgf = LG[:, h:h + 1]
                      lgb = LG[:, 4 + h:5 + h]
                      act(TA[:, :], C("dF4", 0, 128), AF.Exp, [BCFB, BLG, Blnsr], [BTA], bias=lnsr_t[:, 0:1], scale=lgf)
                      act(TB_[:, :], C("dB4", 0, 128), AF.Exp, [BCFB, BLG, Blnsr], [BTB], bias=lnsr_t[:, 0:1], scale=lgb)
                      tt("dve", TA[:, :], TA[:, :], C("MF4", 0, 128), ALU.mult, [BTA, BCF], [BTA])
                      tt("dve", TB_[:, :], TB_[:, :], C("MB4", 0, 128), ALU.mult, [BTB, BCF], [BTB])
                      tt("dve", D4[:, h, :], TA[:, :], TB_[:, :], ALU.add, [BTA, BTB], [BD4])
                      act(WF[:, h, :], C("tp1", 0, 128), AF.Exp, [BCFB, BLG, Blnsr], [BWF], bias=lnsr_t[:, 0:1], scale=lgf)
                      act(WB[:, h, :], C("tm", 0, 128), AF.Exp, [BCFB, BLG, Blnsr], [BWB], bias=lnsr_t[:, 0:1], scale=lgb)
                      act(SFF[:, h, :], C("colA"), AF.Exp, [BCFB, BLG], [BSFF], scale=lgf)
                      act(SBF[:, h, :], C("colB"), AF.Exp, [BCFB, BLG], [BSBF], scale=lgb)
                  UPB, BUPB = sb("UPB", [33, 4, 2, 64], BF16)
                  memset("pool", UPB[:, :, :, :], 0.0, [BUPB])
                  for dr in range(2):
                      dma("pool", UPB[dr * 16:(dr + 1) * 16, :, dr, :],
                          gate_up[layer, dr].rearrange("r (h d) -> r h d", h=4), [], [BUPB])
                      dma("pool", UPB[32:33, :, dr, :], gate_b[layer, dr:dr + 1].rearrange("o (h d) -> o h d", h=4), [], [BUPB])
                  GNB, BGNB = sb("GNB", [128, 128])
                  dma("sp", GNB[:, :], bass.AP(gla_g.tensor, layer * 128, [[0, 128], [1, 128]]), [], [BGNB])

                  chk('K')
                  RS32f, BRS32f = sb("RS32f", [128, 4, 128])
                  RS32b, BRS32b = sb("RS32b", [128, 4, 128])
                  GS32, _ = sb("GS32", [128, 4, 128])
                  BGS32f, BGS32b = Buf("GS32f"), Buf("GS32b")
                  memset("pool", RS32f[:, :, :], 0.0, [BRS32f])
                  memset("pool", RS32b[:, :, :], 0.0, [BRS32b])
                  memset("pool", GS32[:, :, :], 0.0, [BGS32f, BGS32b])
                  RSfbf = [sb(f"RSfbf{i}", [128, 4, 128], BF16) for i in range(2)]
                  RSbbf = [sb(f"RSbbf{i}", [128, 4, 128], BF16) for i in range(2)]
                  GSbf = [sb(f"GSbf{i}", [128, 4, 128], BF16)[0] for i in range(2)]
                  BGSbf_f = [Buf(f"GSbf_f{i}") for i in range(2)]
                  BGSbf_b = [Buf(f"GSbf_b{i}") for i in range(2)]

                  def dbl(name, shape, dt=F32, n=2):
                      l_ = [sb(f"{name}{i}", shape, dt) for i in range(n)]
                      return [l_[i % n] for i in range(6)]

                  X_ = dbl("X", [128, D])
                  XN_ = dbl("XN", [128, D], BF16)
                  HT_ = dbl("HT", [128, 8, 128], BF16)
                  DT_ = dbl("DT", [33, 128], BF16)
                  for i in range(2):
                      memset("pool", DT_[i][0][:, :], 1.0, [DT_[i][1]])
                  L_ = dbl("L", [128, 4, 2, 64])
                  ECUM_ = dbl("ECUM", [128, 2, 4, 64])
                  EINV_ = dbl("EINV", [128, 2, 4, 64])
                  EEND_ = dbl("EEND", [128, 2, 4, 64])
                  GA_ = dbl("GA", [128, 4], n=3)
                  QD_ = dbl("QD", [128, 4, 2, 64], BF16)
                  KI_ = dbl("KI", [128, 4, 2, 64], BF16)
                  KE_ = dbl("KE", [128, 4, 2, 64], BF16, n=3)
                  VG_ = dbl("VG", [128, 4, 128], BF16)
                  GTT_ = dbl("GTT", [128, 2, 4, 128], BF16)
                  PTf_ = dbl("PTf", [128, 4, 128], BF16)
                  PTb_ = dbl("PTb", [128, 4, 128], BF16)
                  SGg_ = dbl("SGg", [128, 4, 128], BF16)
                  GG_ = dbl("GG", [128, 4, 128], BF16)
                  SGr_ = dbl("SGr", [128, 4, 128], BF16)
                  RTA = [sb(f"RTA{i}", [128, 256]) for i in range(2)]
                  RTB = [sb(f"RTB{i}", [128, 256]) for i in range(2)]
                  QR_ = dbl("QR", [128, 4, 128], BF16)
                  KR_ = dbl("KR", [128, 4, 128], BF16)
                  RTq_ = dbl("RTq", [128, 4, 128], BF16)
                  RTqf_ = dbl("RTqf", [128, 4, 128], BF16)
                  RTqb_ = dbl("RTqb", [128, 4, 128], BF16)
                  RTk_ = dbl("RTk", [128, 4, 128], BF16)
                  VR_ = dbl("VR", [128, 4, 128], BF16)
                  VV_ = dbl("VV", [128, 4, 2, 128], BF16)
                  PTr_ = dbl("PTr", [128, 4, 128], BF16)
                  OC_ = dbl("OC", [128, 4, 128], n=1)
                  ST_ = dbl("ST", [128, 16])
                  Y_ = dbl("Y", [128, 8, 128], BF16, n=1)
                  YT_ = dbl("YT", [128, 8, 128], BF16, n=1)
                  if _os.environ.get("KDBG"):
                      print("sbuf remaining (mixer phase):", nc.sbuf_bytes_remaining)
                  Bx = [Buf(f"xtile{i}") for i in range(NT)]
                  Bxm = [Buf(f"xmid{i}") for i in range(NT)]
                  Bsbd = [Buf(f"sbd{i}") for i in range(NT)]

                  def silu_evac(dst, Bdst, pbank, Lbuf):
                      EG = Lbuf[0][:, :, :, :].rearrange("p h r d -> p (h r d)")
                      BEG = Lbuf[1]
                      act(EG, pf32(pbank)[:, :], AF.Exp, [PB(pbank)], [BEG], scale=-1.0)
                      act(EG, EG, AF.Ln, [BEG, Bone], [BEG], bias=one_t[:, 0:1], scale=1.0)
                      act(EG, EG, AF.Exp, [BEG], [BEG], scale=-1.0)
                      tt("dve", dst, pf32(pbank)[:, :], EG, ALU.mult, [PB(pbank), BEG], [Bdst])

                  X_A = [X_[0], X_[1], XM0, XM1]
                  XN_A = [XN_[0], XN_[1],
                          (Y_[0][0][:, :, :].rearrange("p j t -> p (j t)"), Y_[0][1]),
                          (YT_[0][0][:, :, :].rearrange("p j t -> p (j t)"), YT_[0][1])]
                  HT_A = [HT_[0], HT_[1]] + [(GTT_[i][0][:, :, :, :].rearrange("p a h t -> p (a h) t"), GTT_[i][1]) for i in range(2)]
                  L_A = [L_[0], L_[1]] + [(ECUM_[i][0][:, :, :, :].rearrange("p r h d -> p (r h d)").rearrange(
                      "p (h r d) -> p h r d", h=4, r=2), ECUM_[i][1]) for i in range(2)]
                  EEND_A = [EEND_[0], EEND_[1], EINV_[0], EINV_[1]]

                  def tile_proc(ti, nidx, mode):
                      par = nidx % 6
                      is_ctx = ti < 2
                      w = 1 if is_ctx else 0
                      need_q = mode == "B"
                      lists = {"front": [], "back": []}

                      nf_ = NFRONT_A if mode in ("A", "STORE") else NFRONT
                      pools["front"] = list(range(nf_))
                      pools["back"] = list(range(nf_, 8))

                      def sec(name):
                          P.defer = lists[name]
                          rot["cur"] = name

                      if mode in ("A", "STORE"):
                          sec("back")
                          store_entering(ti)
                      sec("front")
                      if mode == "STORE":
                          P.defer = None
                          rot["cur"] = None
                          return lists
                      deepA = mode == "A" and DEEP_A
                      X, BX = X_A[nidx % 4] if deepA else X_[par]
                      dma("sp", X[:, :], xsrc[ti * 128:(ti + 1) * 128, :], [Bx[ti]], [BX])
                      msa, Bm = ms_slot()
                      XN, BXN = XN_A[nidx % 4] if deepA else XN_[par]
                      act(XN[:, :], X[:, :], AF.Square, [BX], [BXN, Bm], scale=1.0 / 32, accum=msa)
                      rstd_from(msa, [Bm], 1.0)
                      ts("dve", XN[:, :], X[:, :], msa, None, ALU.mult, None, [BX, Bm], [BXN])
                      ptx = palloc()
                      txv = pbf(ptx, None)
                      for j in range(8):
                          tr(txv[:, j * 128:(j + 1) * 128], XN[:, j * 128:(j + 1) * 128], identb[:, :], [BXN, Bidentb], [PB(ptx)])
                      HT, BHT = HT_A[nidx % 4] if deepA else HT_[par]
                      for j in range(8):
                          if j % 2 == 0:
                              ts("dve", HT[:, j, :], txv[:, j * 128:(j + 1) * 128], GS[:, 0, j, w:w + 1],
                                 MOD[:, j, w:w + 1], ALU.mult, ALU.add, [PB(ptx), BGS, BMOD], [BHT])
                          else:
                              act(HT[:, j, :], txv[:, j * 128:(j + 1) * 128], AF.Identity, [PB(ptx), BGS, BMOD], [BHT],
                                  bias=MOD[:, j, w:w + 1], scale=GS[:, 0, j, w:w + 1])
                      pfree(ptx)

                      def wbufs(k, c0, c1):
                          return [BWINp[(k, p0)] for p0 in (0, 1024, 2048, 3072) if p0 < c1 and min(p0 + 1024, IN_W) > c0]

                      def proj(c0, c1, pbank, col0=0, M=None):
                          n = c1 - c0
                          for k in range(8):
                              mm(pf32(pbank)[:, col0:col0 + n], HT[:, k, :], WIN[:, k, c0:c1], k == 0, k == 7,
                                 [BHT] + wbufs(k, c0, c1), [PB(pbank)])

                      prk = palloc()
                      for k in range(8):
                          mm(pf32(prk)[0:32, 0:128], WIN[:, k, 3584:3616], HT[:, k, :], k == 0, k == 7,
                             [BHT] + wbufs(k, 3584, 3616), [PB(prk)])
                      DT, BDT = DT_[par]
                      cp("act", DT[0:32, :], pf32(prk)[0:32, 0:128], [PB(prk)], [BDT])
                      pfree(prk)
                      pz = palloc()
                      mm(pf32(pz)[:, :], DT[:, :], UPB[:, :, :, :].rearrange("p h r d -> p (h r d)"), True, True,
                         [BDT, BUPB], [PB(pz)])
                      L, BL = L_A[nidx % 4] if deepA else L_[par]
                      Lflat = L[:, :, :, :].rearrange("p h r d -> p (h r d)")
                      act(Lflat, pf32(pz)[:, :], AF.Exp, [PB(pz)], [BL], scale=-1.0)
                      pfree(pz)
                      EEND, BEEND = EEND_A[nidx % 4] if deepA else EEND_[par]
                      LBflat = EEND[:, :, :, :].rearrange("p r h d -> p (r h d)").bitcast(BF16)[:, 0:512]
                      LB = LBflat.rearrange("p (h r d) -> p h r d", h=4, r=2)
                      act(LBflat, Lflat, AF.Ln, [BL, Bone], [BEEND], bias=one_t[:, 0:1], scale=1.0)
                      pcum = palloc()
                      prem = palloc()
                      mm(pf32(pcum)[:, 0:256], CB("LTf"), LB[:, :, 0, :], True, True, [BCB16, BEEND], [PB(pcum)])
                      mm(pf32(pcum)[:, 256:512], CB("LTb"), LB[:, :, 1, :], True, True, [BCB16, BEEND], [PB(pcum)])
                      mm(pf32(prem)[:, 0:256], CB("LSf"), LB[:, :, 0, :], True, True, [BCB16, BEEND], [PB(prem)])
                      mm(pf32(prem)[:, 256:512], CB("LSb"), LB[:, :, 1, :], True, True, [BCB16, BEEND], [PB(prem)])
                      ECUM, BECUM = ECUM_[par]
                      EINV, BEINV = EINV_[par]
                      EEND, BEEND = EEND_A[nidx % 4] if deepA else EEND_[par]
                      fl = lambda t_: t_[:, :, :, :].rearrange("p r h d -> p (r h d)")
                      if need_q:
                          act(fl(ECUM), pf32(pcum)[:, :], AF.Exp, [PB(pcum), Blnsg], [BECUM], bias=lnsg_t[:, 0:1], scale=1.0)
                          act(fl(EINV), pf32(pcum)[:, :], AF.Exp, [PB(pcum)], [BEINV], scale=-1.0)
                      pfree(pcum)
                      pga = palloc()
                      for h in range(4):
                          mm(pf32(pga)[:, 2 * h:2 * h + 2], LB[:, h, :, :].rearrange("p r d -> p (r d)"), CB("neg16"), True, True,
                             [BEEND, BCB16], [PB(pga)])
                      GA, BGA = GA_[par]
                      act(GA[:, :], pf32(pga)[:, 0:8].rearrange("p (h t) -> p h t", t=2)[:, :, 0], AF.Exp, [PB(pga)], [BGA])
                      pfree(pga)
                      act(fl(EEND), pf32(prem)[:, :], AF.Exp, [PB(prem)], [BEEND])
                      pfree(prem)

                      if need_q:
                          pgg = palloc()
                          proj(3072, 3584, pgg)
                          SGg, BSGg = SGg_[par]
                          silu_evac(SGg[:, :, :].rearrange("p h d -> p (h d)"), BSGg, pgg, L_[par])
                          pfree(pgg)
                          GG, BGG = GG_[par]
                          tt("pool", GG[:, :, :], SGg[:, :, :], bcast_mid(GNB[:, :], 4), ALU.mult, [BSGg, BGNB], [BGG])
                          if mode == 'B' and ti == 0: chk('b1')

                      pg3 = palloc()
                      if need_q:
                          proj(2048, 2560, pg3)
                      else:
                          proj(2304, 2560, pg3, col0=256)
                      q3 = pf32(pg3)[:, 0:256].rearrange("p (h d) -> p h d", h=4)
                      k3 = pf32(pg3)[:, 256:512].rearrange("p (h d) -> p h d", h=4)
                      QD, BQD = QD_[par]
                      KI, BKI = KI_[par]
                      KE, BKE = KE_[par]
                      for dr in range(2):
                          if need_q:
                              tt("dve", QD[:, :, dr, :], q3, ECUM[:, dr, :, :], ALU.mult, [PB(pg3), BECUM], [BQD])
                              tt("dve", KI[:, :, dr, :], k3, EINV[:, dr, :, :], ALU.mult, [PB(pg3), BEINV], [BKI])
                          tt("dve", KE[:, :, dr, :], k3, EEND[:, dr, :, :], ALU.mult, [PB(pg3), BEEND], [BKE])
                      pfree(pg3)
                      pgv = palloc()
                      proj(2560, 3072, pgv)
                      VG, BVG = VG_[par]
                      cp("act", VG[:, :, :].rearrange("p h d -> p (h d)"), pf32(pgv)[:, :], [PB(pgv)], [BVG])
                      pfree(pgv)
                      if mode == 'B' and ti == 0: chk('b2')

                      GSb, BGf, BGb = GSbf[nidx % 2], BGSbf_f[nidx % 2], BGSbf_b[nidx % 2]
                      sec("back")
                      if need_q:
                          RSb, BRSb = RSbbf[nidx % 2]
                          dma("sp", RSb[:, :, :].rearrange("p h d -> p (h d)"), sbd[ti, :, 0:512], [Bsbd[ti]], [BRSb])
                          dma("sp", GSb[64:128, :, :].rearrange("p h d -> p (h d)"), sbd[ti, 64:128, 512:1024],
                              [Bsbd[ti]], [BGb])
                          sec("front")
                          ptg = palloc()
                          tgv = pbf(ptg, None).rearrange("p (a h t) -> p a h t", a=2, h=4)
                          for h in range(4):
                              tr(tgv[:, 0, h, :], QD[:, h, :, :].rearrange("p r d -> p (r d)"), identb[:, :],
                                 [BQD, Bidentb], [PB(ptg)])
                          for h in range(4):
                              tr(tgv[:, 1, h, :], KI[:, h, :, :].rearrange("p r d -> p (r d)"), identb[:, :],
                                 [BKI, Bidentb], [PB(ptg)])
                          GTT, BGTT = GTT_[par]
                          cp("act", GTT[:, 0, :, :], tgv[:, 0, :, :], [PB(ptg)], [BGTT])
                          cp("dve", GTT[:, 1, :, :], tgv[:, 1, :, :], [PB(ptg)], [BGTT])
                          pfree(ptg)
                          if mode == 'B' and ti == 0: chk('b3')
                          psf = palloc()
                          psb = palloc()
                          for h in range(4):
                              mm(pf32(psf)[:, h * 128:(h + 1) * 128], GTT[0:64, 1, h, :], GTT[0:64, 0, h, :], True, True,
                                 [BGTT], [PB(psf)])
                          for h in range(4):
                              mm(pf32(psb)[:, h * 128:(h + 1) * 128], GTT[64:128, 1, h, :], GTT[64:128, 0, h, :], True, True,
                                 [BGTT], [PB(psb)])
                          PTf, BPTf = PTf_[par]
                          PTb, BPTb = PTb_[par]
                          tt("dve", PTf[:, :, :], pf32(psf)[:, :].rearrange("p (h t) -> p h t", h=4), bcast_mid(C("MF4"), 4), ALU.mult,
                             [PB(psf), BCF], [BPTf])
                          tt("dve", PTb[:, :, :], pf32(psb)[:, :].rearrange("p (h t) -> p h t", h=4), bcast_mid(C("MB4"), 4), ALU.mult,
                             [PB(psb), BCF], [BPTb])
                          pfree(psf)
                          pfree(psb)
                          sec("back")
                          pog = palloc()
                          for h in range(4):
                              o_ap = pf32(pog)[:, h * 128:(h + 1) * 128]
                              mm(o_ap, PTf[:, h, :], VG[:, h, :], True, False, [BPTf, BVG], [PB(pog)])
                              mm(o_ap, PTb[:, h, :], VG[:, h, :], False, False, [BPTb, BVG], [PB(pog)])
                              mm(o_ap, GTT[:, 0, h, :], GSb[:, h, :], False, True, [BGTT, BGf, BGb], [PB(pog)])
                          ST, BST = ST_[par]
                          Y, BY = Y_[par]
                          for h in range(4):
                              act(Y[:, 4 + h, :], pf32(pog)[:, h * 128:(h + 1) * 128], AF.Square, [PB(pog)], [BY, BST],
                                  accum=ST[:, 8 + h:9 + h])
                          rstd_from(ST[:, 8:12], [BST], 1.0 / 128)
                          for h in range(4):
                              stt(Y[:, 4 + h, :], pf32(pog)[:, h * 128:(h + 1) * 128], ST[:, 8 + h:9 + h], GG[:, h, :],
                                  ALU.mult, ALU.mult, [PB(pog), BST, BGG], [BY])
                          pfree(pog)
                          if mode == 'B' and ti == 0: chk('b5')
                      pgu = palloc()
                      for h in range(4):
                          mm(pf32(pgu)[:, h * 128:(h + 1) * 128], KE[:, h, :, :].rearrange("p r d -> p (r d)"), VG[:, h, :],
                             True, True, [BKE, BVG], [PB(pgu)])
                      if mode == "A":
                          lo, hi, Bst = 64, 128, BGS32b
                      else:
                          lo, hi, Bst = 0, 64, BGS32f
                      npar = (nidx + 1) % 2
                      for h in range(4):
                          stt(GS32[lo:hi, h, :], GS32[lo:hi, h, :], GA[lo:hi, h:h + 1], pf32(pgu)[lo:hi, h * 128:(h + 1) * 128],
                              ALU.mult, ALU.add, [Bst, BGA, PB(pgu)], [Bst])
                      pfree(pgu)
                      if mode != "A":
                          cp("act", GSbf[npar][0:64, :, :], GS32[0:64, :, :], [BGS32f], [BGSbf_f[npar]])
                          if mode == 'B' and ti == 0: chk('b6')

                      sec("front")
                      if need_q:
                          pgr = palloc()
                          proj(1536, 2048, pgr)
                          SGr, BSGr = SGr_[par]
                          silu_evac(SGr[:, :, :].rearrange("p h d -> p (h d)"), BSGr, pgr, L_[par])
                          pfree(pgr)
                      QR, BQR = QR_[par]
                      KR, BKR = KR_[par]
                      COS, BCOS = COS_[nidx % 2]
                      SIN, BSIN = SIN_[nidx % 2]
                      if not is_ctx:
                          dma("sp", COS[:, :], cosd[:, ti - 2, :], [], [BCOS])
                          dma("sp", SIN[:, :], sind[:, ti - 2, :], [], [BSIN])

                      RT1, BRT1 = RTA[nidx % 2]
                      RT2, BRT2 = RTB[nidx % 2]
                      RT3, BRT3 = RT1, BRT1
                      RT4, BRT4 = RT2, BRT2

                      def rotary(pbank, dst, Bdst):
                          if is_ctx:
                              cp("act", dst[:, :, :].rearrange("p h d -> p (h d)"), pf32(pbank)[:, :], [PB(pbank)], [Bdst])
                              return
                          lt = ti - 2
                          v5 = pf32(pbank)[:, :].rearrange("p (h a b f) -> p h a b f", h=4, a=2, b=2)
                          d5 = dst[:, :, :].rearrange("p h (a b f) -> p h a b f", a=2, b=2)
                          u1 = v5[:, :, :, 0, :]
                          u2 = v5[:, :, :, 1, :]
                          cosb = bcast_mid(COS[:, :].rearrange("p (a f) -> p a f", a=2), 4)
                          sinb = bcast_mid(SIN[:, :].rearrange("p (a f) -> p a f", a=2), 4)
                          r4 = lambda t_: t_[:, :].rearrange("p (h a f) -> p h a f", h=4, a=2)
                          tt("dve", r4(RT1), u1, cosb, ALU.mult, [PB(pbank), BCOS], [BRT1])
                          tt("dve", r4(RT2), u2, sinb, ALU.mult, [PB(pbank), BSIN], [BRT2])
                          tt("pool", d5[:, :, :, 0, :], r4(RT1), r4(RT2), ALU.subtract, [BRT1, BRT2], [Bdst])
                          tt("dve", r4(RT3), u1, sinb, ALU.mult, [PB(pbank), BSIN], [BRT3])
                          tt("dve", r4(RT4), u2, cosb, ALU.mult, [PB(pbank), BCOS], [BRT4])
                          tt("pool", d5[:, :, :, 1, :], r4(RT3), r4(RT4), ALU.add, [BRT3, BRT4], [Bdst])

                      if need_q:
                          pq = palloc()
                          proj(0, 512, pq)
                          rotary(pq, QR, BQR)
                          pfree(pq)
                      pk = palloc()
                      proj(512, 1024, pk)
                      rotary(pk, KR, BKR)
                      pfree(pk)
                      pv = palloc()
                      proj(1024, 1536, pv)
                      VR, BVR = VR_[par]
                      VV, BVV = VV_[par]
                      cp("act", VR[:, :, :].rearrange("p h d -> p (h d)"), pf32(pv)[:, :], [PB(pv)], [BVR])
                      pfree(pv)
                      tt("pool", VV[:, :, 0, :], VR[:, :, :], SFF[:, :, :], ALU.mult, [BVR, BSFF], [BVV])
                      tt("pool", VV[:, :, 1, :], VR[:, :, :], SBF[:, :, :], ALU.mult, [BVR, BSBF], [BVV])
                      if mode == 'B' and ti == 0: chk('b7')
                      if need_q:
                          ptr_ = palloc()
                          trv = pbf(ptr_, None).rearrange("p (a h t) -> p a h t", a=2, h=4)
                          for h in range(4):
                              tr(trv[:, 0, h, :], QR[:, h, :], identb[:, :], [BQR, Bidentb], [PB(ptr_)])
                          for h in range(4):
                              tr(trv[:, 1, h, :], KR[:, h, :], identb[:, :], [BKR, Bidentb], [PB(ptr_)])
                          RTq, BRTq = RTq_[par]
                          RTqf, BRTqf = RTqf_[par]
                          RTqb, BRTqb = RTqb_[par]
                          RTk, BRTk = RTk_[par]
                          if mode == 'B' and ti == 0: chk('b7a')
                          cp("act", RTq[:, :, :], trv[:, 0, :, :], [PB(ptr_)], [BRTq])
                          if mode == 'B' and ti == 0: chk('b7b')
                          cp("act", RTk[:, :, :], trv[:, 1, :, :], [PB(ptr_)], [BRTk])
                          if mode == 'B' and ti == 0: chk('b7c')
                          tt("pool", RTqf[:, :, :], RTq[:, :, :], WF[:, :, :], ALU.mult, [BRTq, BWF], [BRTqf])
                          if mode == 'B' and ti == 0: chk('b7d')
                          tt("pool", RTqb[:, :, :], RTq[:, :, :], WB[:, :, :], ALU.mult, [BRTq, BWB], [BRTqb])
                          pfree(ptr_)
                          if mode == 'B' and ti == 0: chk('b8')
                          psr = palloc()
                          for h in range(4):
                              mm(pf32(psr)[:, h * 128:(h + 1) * 128], RTk[:, h, :], RTq[:, h, :], True, True,
                                 [BRTk, BRTq], [PB(psr)])
                          PTr, BPTr = PTr_[par]
                          tt("dve", PTr[:, :, :].rearrange("p h d -> p (h d)"), pf32(psr)[:, :],
                             D4[:, :, :].rearrange("p h d -> p (h d)"), ALU.mult, [PB(psr), BD4], [BPTr])
                          pfree(psr)
                      sec("back")
                      if need_q:
                          por = palloc()
                          RSf, BRSf = RSfbf[nidx % 2]
                          for h in range(4):
                              o_ap = pf32(por)[:, h * 128:(h + 1) * 128]
                              mm(o_ap, PTr[:, h, :], VR[:, h, :], True, False, [BPTr, BVR], [PB(por)])
                              mm(o_ap, RTqf[:, h, :], RSf[:, h, :], False, False, [BRTqf, BRSf], [PB(por)])
                              mm(o_ap, RTqb[:, h, :], RSb[:, h, :], False, True, [BRTqb, BRSb], [PB(por)])
                          orv = pf32(por)[:, :].rearrange("p (h d) -> p h d", h=4)
                          OC, BOC = OC_[par]
                          P.op("dve", lambda e: e.tensor_reduce(out=ST[:, 0:4], in_=orv, axis=AX.X, op=ALU.add),
                               reads=[PB(por)], writes=[BST])
                          ts("dve", ST[:, 0:4], ST[:, 0:4], -1.0 / 128, None, ALU.mult, None, [BST], [BST])
                          for h in range(4):
                              act(OC[:, h, :], pf32(por)[:, h * 128:(h + 1) * 128], AF.Square, [PB(por), BST], [BOC, BST],
                                  bias=ST[:, h:h + 1], scale=1.0, accum=ST[:, 4 + h:5 + h])
                          rstd_from(ST[:, 4:8], [BST], 1.0 / 128)
                          for h in range(4):
                              ts("dve", OC[:, h, :], pf32(por)[:, h * 128:(h + 1) * 128], ST[:, h:h + 1], ST[:, 4 + h:5 + h],
                                 ALU.add, ALU.mult, [PB(por), BST], [BOC])
                          pfree(por)
                          tt("pool", Y[:, 0:4, :], OC[:, :, :], SGr[:, :, :], ALU.mult, [BOC, BSGr], [BY])
                          if mode == 'B' and ti == 0: chk('b9')
                      pru0 = palloc()
                      pru1 = palloc()
                      for h in range(4):
                          pb_ = pru0 if h < 2 else pru1
                          mm(pf32(pb_)[:, (h % 2) * 256:(h % 2) * 256 + 256], KR[:, h, :],
                             VV[:, h, :, :].rearrange("p r d -> p (r d)"), True, True, [BKR, BVV], [PB(pb_)])
                      if mode == "A":
                          for h in range(4):
                              pb_ = pru0 if h < 2 else pru1
                              stt(RS32b[:, h, :], RS32b[:, h, :], G128[:, 4 + h:5 + h],
                                  pf32(pb_)[:, (h % 2) * 256 + 128:(h % 2) * 256 + 256], ALU.mult, ALU.add,
                                  [BRS32b, BG128, PB(pb_)], [BRS32b])
                      else:
                          for h in range(4):
                              pb_ = pru0 if h < 2 else pru1
                              stt(RS32f[:, h, :], RS32f[:, h, :], G128[:, h:h + 1],
                                  pf32(pb_)[:, (h % 2) * 256:(h % 2) * 256 + 128], ALU.mult, ALU.add,
                                  [BRS32f, BG128, PB(pb_)], [BRS32f])
                          cp("act", RSfbf[npar][0][:, :, :], RS32f[:, :, :], [BRS32f], [RSfbf[npar][1]])
                      pfree(pru0)
                      pfree(pru1)
                      if mode == 'B' and ti == 0: chk('b10')

                      if need_q:
                          pty = palloc()
                          tyv = pbf(pty, None).rearrange("p (j t) -> p j t", j=8)
                          for j in range(8):
                              tr(tyv[:, j, :], Y[:, j, :], identb[:, :], [BY, Bidentb], [PB(pty)])
                          YT, BYT = YT_[par]
                          cp("act", YT[:, 0:4, :], tyv[:, 0:4, :], [PB(pty)], [BYT])
                          cp("dve", YT[:, 4:8, :], tyv[:, 4:8, :], [PB(pty)], [BYT])
                          pfree(pty)
                          TMP, BTMP = TMP_[par]
                          XM, BXM = XM_[par]
                          dma("sp", XM[:, :], xsrc[ti * 128:(ti + 1) * 128, :], [Bx[ti]], [BXM])
                          for nb in range(2):
                              po = palloc()
                              for k in range(8):
                                  mm(pf32(po)[:, :], YT[:, k, :], WOUT[:, k, nb * 512:(nb + 1) * 512], k == 0, k == 7,
                                     [BYT, BWOk[k]], [PB(po)])
                              if is_ctx:
                                  tt("dve", TMP[:, nb * 512:(nb + 1) * 512], pf32(po)[:, :], G1[:, nb * 512:(nb + 1) * 512],
                                     ALU.mult, [PB(po), BG1], [BTMP])
                              else:
                                  tt("dve", XM[:, nb * 512:(nb + 1) * 512], pf32(po)[:, :], XM[:, nb * 512:(nb + 1) * 512],
                                     ALU.add, [PB(po), BXM], [BXM])
                              pfree(po)
                          if is_ctx:
                              tt("pool", XM[:, :], TMP[:, :], XM[:, :], ALU.add, [BTMP, BXM], [BXM])
                          dma("sp", xmid[ti * 128:(ti + 1) * 128, :], XM[:, :], [BXM], [Bxm[ti]])
                      P.defer = None
                      rot["cur"] = None
                      return lists

                  RBc, BRBc = sb("RBc", [128, 4, 128], BF16)
                  GBc, BGBc = sb("GBc", [128, 4, 128], BF16)

                  def store_entering(ti):
                      cp("act", RBc[:, :, :], RS32b[:, :, :], [BRS32b], [BRBc])
                      cp("act", GBc[64:128, :, :], GS32[64:128, :, :], [BGS32b], [BGBc])
                      dma("sp", sbd[ti, :, 0:512], RBc[:, :, :].rearrange("p h d -> p (h d)"), [BRBc], [Bsbd[ti]])
                      dma("sp", sbd[ti, 64:128, 512:1024], GBc[64:128, :, :].rearrange("p h d -> p (h d)"), [BGBc], [Bsbd[ti]])

                  seq_b = [1, 0] + list(range(NT - 1, 1, -1))
                  items = [(ti, "A") for ti in seq_b[:-1]] + [(seq_b[-1], "STORE")]
                  nA = len(items)
                  for ti in range(NT):
                      items.append((ti, "S" if (last and ti < 2) else "B"))
                  cp("act", RSfbf[nA % 2][0][:, :, :], RS32f[:, :, :], [BRS32f], [RSfbf[nA % 2][1]])
                  cp("act", GSbf[nA % 2][0:64, :, :], GS32[0:64, :, :], [BGS32f], [BGSbf_f[nA % 2]])
                  def rescale_wout():
                      gate_tile(G1, BG1, 2, 0)
                      for k in range(8):
                          tt("pool", WOUT[:, k, :], WOUT[:, k, :], G1[:, :], ALU.mult, [BWOk[k], BG1], [BWOk[k]])

                  if last:
                      rescale_wout()
                  prev_back = None
                  prev_item = None
                  for n, (ti, mode) in enumerate(items):
                      lists = tile_proc(ti, n, mode)
                      P.replay(lists["front"], 1)
                      if prev_back is not None:
                          P.replay(prev_back, 0)
                          if prev_item == (1, "B"):
                              rescale_wout()
                      prev_back = lists["back"]
                      prev_item = (ti, mode)
                  P.replay(prev_back)
                  chk('B')
                  P.fence()

              with ExitStack() as ph:
                  sb = mk(ph)
                  W1, _ = sb("W1", [128, 8, DFF], BF16)
                  W2, _ = sb("W2", [128, 32, D], BF16)
                  BW1 = [[Buf(f"W1_{k}_{q}") for q in range(4)] for k in range(8)]
                  BW2 = [Buf(f"W2_{k}") for k in range(32)]
                  for q in range(4):
                      for k in range(8):
                          dma("pool", W1[:, k, q * 1024:(q + 1) * 1024], w1[layer, k * 128:(k + 1) * 128, q * 1024:(q + 1) * 1024],
                              [], [BW1[k][q]])
                  w2_last = None
                  for k in range(32):
                      w2_last = dma("pool", W2[:, k, :], w2[layer, k * 128:(k + 1) * 128, :], [], [BW2[k]])
                  identf2, Bidf2 = sb("identf2", [128, 128])
                  onesf2, Bonf2 = sb("onesf2", [128, 128])
                  o_, w_ = cols["identf"]
                  dma("sp", identf2[:, :], cf[:, o_:o_ + 128], [], [Bidf2])
                  o_, w_ = cols["onesf"]
                  dma("sp", onesf2[:, :], cf[:, o_:o_ + 128], [], [Bonf2])
                  G2x, BG2x = sb("G2x", [128, D])
                  G2c, BG2c = sb("G2c", [128, D])
                  DG2, BDG2 = sb("DG2", [128, 128])

                  def gate_tile2(dst, Bdst, m, w):
                      for j in range(8):
                          ts("dve", DG2[:, :], identf2[:, :], MOD[:, m * 8 + j, w:w + 1], None, ALU.mult, None,
                             [Bidf2, BMOD], [BDG2])
                          pb_ = palloc()
                          mm(pf32(pb_)[:, 0:128], onesf2[:, :], DG2[:, :], True, True, [Bonf2, BDG2], [PB(pb_)])
                          cp("act", dst[:, j * 128:(j + 1) * 128], pf32(pb_)[:, 0:128], [PB(pb_)], [Bdst])
                          pfree(pb_)

                  gate_tile2(G2x, BG2x, 5, 0)
                  if not last:
                      gate_tile2(G2c, BG2c, 5, 1)
                  else:
                      dma("sp", G2c[:, :], bass.AP(final_g.tensor, 0, [[0, 128], [1, D]]), [], [BG2c])

                  if layer + 1 < DEPTH and not _os.environ.get("KNOOVL"):
                      rot["banks"] = list(range(7))
                      bank_rr[0] = 0
                      ADA2 = [sb(f"ADA2{i}", [128, 8, 128]) for i in range(2)]
                      phase_M(layer + 1, [(t_[:, :, :], b_) for (t_, b_) in ADA2], sb, after=(w2_last,))
                  if _os.environ.get("KDBG"):
                      print("sbuf remaining (mlp phase, before work bufs):", nc.sbuf_bytes_remaining)
                  NXS = 4
                  XS = [sb(f"XS{i}", [128, D]) for i in range(NXS)]
                  XN2 = [sb(f"XN2{i}", [128, D], BF16) for i in range(2)]
                  HT2 = [sb(f"HT2{i}", [128, 8, 256], BF16) for i in range(2)]
                  HID0 = sb("HID", [128, 32, 256], BF16)
                  HID = [HID0, HID0]
                  RL = [sb(f"RL{i}", [128, 256]) for i in range(2)]
                  TM20 = sb("TM2", [128, D])
                  TM2 = [TM20, TM20]
                  XO = [sb(f"XO{i}", [128, D]) for i in range(2)]
                  Bxn = [Buf(f"xnext{i}") for i in range(NT)]
                  Bxm2 = [Buf(f"xmid2_{i}") for i in range(NT)]
                  tiles = list(range(2, NT)) if last else list(range(NT))
                  groups = [tiles[i:i + 2] for i in range(0, len(tiles), 2)]
                  xs_rr = 0
                  xn_rr = 0
                  for gi, grp in enumerate(groups):
                      gp = gi % 2
                      HTg, BHTg = HT2[gp]
                      HIDg, BHIDg = HID[gp]
                      xs_of = {}
                      for jj, ti in enumerate(grp):
                          Xs, BXs = XS[xs_rr % NXS]
                          xs_rr += 1
                          xs_of[ti] = (Xs, BXs)
                          w = 1 if ti < 2 else 0
                          dma("sp", Xs[:, :], xmid[ti * 128:(ti + 1) * 128, :], [Bxm2[ti]], [BXs])
                          msa, Bm = ms_slot()
                          XNt, BXNt = XN2[xn_rr % 2]
                          xn_rr += 1
                          act(XNt[:, :], Xs[:, :], AF.Square, [BXs], [BXNt, Bm], scale=1.0 / 32, accum=msa)
                          rstd_from(msa, [Bm], 1.0)
                          ts("dve", XNt[:, :], Xs[:, :], msa, None, ALU.mult, None, [BXs, Bm], [BXNt])
                          ptx = palloc()
                          txv = pbf(ptx, None)
                          for j in range(8):
                              tr(txv[:, j * 128:(j + 1) * 128], XNt[:, j * 128:(j + 1) * 128], identb[:, :],
                                 [BXNt, Bidentb], [PB(ptx)])
                          for j in range(8):
                              if j % 2 == 0:
                                  ts("dve", HTg[:, j, jj * 128:(jj + 1) * 128], txv[:, j * 128:(j + 1) * 128],
                                     GS[:, 1, j, w:w + 1], MOD[:, 24 + j, w:w + 1], ALU.mult, ALU.add,
                                     [PB(ptx), BGS, BMOD], [BHTg])
                              else:
                                  act(HTg[:, j, jj * 128:(jj + 1) * 128], txv[:, j * 128:(j + 1) * 128], AF.Identity,
                                      [PB(ptx), BGS, BMOD], [BHTg], bias=MOD[:, 24 + j, w:w + 1], scale=GS[:, 1, j, w:w + 1])
                          pfree(ptx)
                      ng = len(grp) * 128
                      for hc in range(32):
                          ph1 = palloc()
                          for k in range(8):
                              mm(pf32(ph1)[:, 0:ng], W1[:, k, hc * 128:(hc + 1) * 128], HTg[:, k, 0:ng], k == 0, k == 7,
                                 [BW1[k][hc // 8], BHTg], [PB(ph1)])
                          RLt, BRLt = RL[hc % 2]
                          act(RLt[:, 0:ng], pf32(ph1)[:, 0:ng], AF.Relu, [PB(ph1)], [BRLt])
                          pfree(ph1)
                          tt("dve" if hc % 2 == 0 else "pool", HIDg[:, hc, 0:ng], RLt[:, 0:ng], RLt[:, 0:ng], ALU.mult,
                             [BRLt], [BHIDg])
                      for jj, ti in enumerate(grp):
                          Xs, BXs = xs_of[ti]
                          is_ctx = ti < 2
                          Gt, BGt = (G2c, BG2c) if is_ctx else (G2x, BG2x)
                          TMt, BTMt = TM2[jj % 2]
                          XOt, BXOt = XO[jj % 2]
                          for nb in range(2):
                              po = palloc()
                              for k in range(32):
                                  mm(pf32(po)[:, :], HIDg[:, k, jj * 128:(jj + 1) * 128], W2[:, k, nb * 512:(nb + 1) * 512],
                                     k == 0, k == 31, [BHIDg, BW2[k]], [PB(po)])
                              tt("dve", TMt[:, nb * 512:(nb + 1) * 512], pf32(po)[:, :], Gt[:, nb * 512:(nb + 1) * 512],
                                 ALU.mult, [PB(po), BGt], [BTMt])
                              pfree(po)
                          tt("pool", XOt[:, :], TMt[:, :], Xs[:, :], ALU.add, [BTMt, BXs], [BXOt])
                          if not last:
                              dma("sp", xnext[ti * 128:(ti + 1) * 128, :], XOt[:, :], [BXOt], [Bx[ti]] if False else [Bxn[ti]])
                          else:
                              msa, Bm = ms_slot()
                              act(TMt[:, :], XOt[:, :], AF.Square, [BXOt], [BTMt, Bm], scale=1.0 / 32, accum=msa)
                              rstd_from(msa, [Bm], 1.0)
                              stt(TMt[:, :], XOt[:, :], msa, G2c[:, :], ALU.mult, ALU.mult, [BXOt, Bm, BG2c], [BTMt])
                              dma("sp", out[(ti - 2) * 128:(ti - 1) * 128, :], TMt[:, :], [BTMt], [Bxn[ti]])
                  P.fence()
                  rot["banks"] = list(range(8))
                  bank_rr[0] = 0

          except _Stop:
            break

        P.finalize()
        global _LASTP
        _LASTP = P
        P.alloc_sems(nc, top)
        with nc.Block() as block:
            @block.tensor
            def _(e):
                P.emit_stream("pe", e)

            @block.scalar
            def _(e):
                P.emit_stream("act", e)

            @block.vector
            def _(e):
                P.emit_stream("dve", e)

            @block.gpsimd
            def _(e):
                P.emit_stream("pool", e)

            @block.sync
            def _(e):
                P.emit_stream("sp", e)
                for c, k in P.chan_cnt.items():
                    e.wait_ge(P.csems[c], 16 * k)
    return nc


_CACHE = {}


def _tT(v, n):
    return np.ascontiguousarray(np.asarray(v, np.float32).reshape(n, 128).T)


def kernel(x, c, ctx, c_ctx, ada_w, ada_b, norm1_g, w_in, ret_decay, gla_gate_up, gla_gate_b,
           gla_norm_g, w_out, norm2_g, w_mlp1, w_mlp2, final_g):
    x = np.asarray(x, np.float32)
    B, S, _ = x.shape
    nlat = S // 128
    arr, cols, cosT, sinT = _const_f32(nlat)
    key = nlat
    if key not in _CACHE:
        _CACHE[key] = build(nlat, cols, arr.shape[1])
    nc = _CACHE[key]
    f = lambda a: np.ascontiguousarray(np.asarray(a, np.float32))
    shared = {
        "ada_w": f(ada_w),
        "ada_bT": np.stack([_tT(np.asarray(ada_b)[l], 48) for l in range(DEPTH)]),
        "n1gT": np.stack([_tT(np.asarray(norm1_g)[l], 8) for l in range(DEPTH)]),
        "n2gT": np.stack([_tT(np.asarray(norm2_g)[l], 8) for l in range(DEPTH)]),
        "w_in": f(w_in),
        "ret_decay": f(ret_decay).reshape(DEPTH, 8),
        "gate_up": f(gla_gate_up),
        "gate_b": f(gla_gate_b),
        "gla_g": f(gla_norm_g),
        "w_out": f(w_out),
        "w1": f(w_mlp1),
        "w2": f(w_mlp2),
        "final_g": f(final_g),
        "cf": arr,
        "cosd": cosT,
        "sind": sinT,
        "identb_d": np.eye(128).astype(ml_dtypes.bfloat16),
        "cb16": CB_ARR,
    }
    ctx = np.asarray(ctx, np.float32)
    c = np.asarray(c, np.float32)
    c_ctx = np.asarray(c_ctx, np.float32)
    in_maps = []
    for b in range(B):
        m = dict(shared)
        m["xin"] = np.ascontiguousarray(np.concatenate([ctx[b], x[b]], axis=0))
        m["cT"] = np.ascontiguousarray(np.concatenate([_tT(c[b], 8), _tT(c_ctx, 8)], axis=1))
        in_maps.append(m)
    res = run_bass_kernel_spmd(nc, in_maps, core_ids=list(range(B)))
    return np.stack([np.asarray(r["out"], np.float32) for r in res.results], axis=0)
```
